# Optimizing a Trainium2 kernel written in Bass

```python
import math
import jax, jax.numpy as jnp
from jax import lax
import numpy as np

D_MODEL = 4096
BATCH = 4
SEQ = 2048
DEPTH = 2
DEC_BATCH = 8
DEC_SEQ = 8
PAST_LEN = 16384
PAGE_SIZE = 128

HEAD_DIM = 128
HEADS_PER_GROUP = 8
ATTN_GROUPS = ((128, 1), (512, 4), (2048, 16))
N_ATTN_GROUPS = len(ATTN_GROUPS)
N_HEADS = HEADS_PER_GROUP * N_ATTN_GROUPS
ATTN_WIDTH = N_HEADS * HEAD_DIM
GROUP_WIDTH = HEADS_PER_GROUP * HEAD_DIM
BLK = 128
POOL_WINDOWS = (2, 4, 8, 16)
POOL_WIDTH = D_MODEL // 4
POOL_GROUP = POOL_WIDTH // len(POOL_WINDOWS)
POOL_STATE = max(POOL_WINDOWS) - 1
N_BUCKETS = 32
MAX_EXACT = 16
MAX_DISTANCE = 2048
D_FF = 4 * D_MODEL
IN_WIDTH = POOL_WIDTH + 3 * ATTN_WIDTH + 2 * D_MODEL
EPS = 1e-6

kernel_name = 'hybrid_pool_dilated_attn_decoder_step'


def rmsnorm(x, g):
    x32 = x.astype(jnp.float32)
    r = x32 * lax.rsqrt(jnp.mean(x32 * x32, axis=-1, keepdims=True) + EPS)
    return (r * g.astype(jnp.float32)).astype(x.dtype)


def t5_bucket(dist):
    small = dist < MAX_EXACT
    nf = jnp.maximum(dist, 1).astype(jnp.float32)
    large = MAX_EXACT + (jnp.log(nf / MAX_EXACT) / math.log(MAX_DISTANCE / MAX_EXACT)
                         * (N_BUCKETS - MAX_EXACT)).astype(jnp.int32)
    large = jnp.minimum(large, N_BUCKETS - 1)
    return jnp.where(small, dist, large)


def group_bias(rel_bias, g):
    w, d = ATTN_GROUPS[g]
    dist = jnp.arange(w // d + 1, dtype=jnp.int32) * d
    tbl = rel_bias[t5_bucket(dist)]
    return tbl[:, g * HEADS_PER_GROUP:(g + 1) * HEADS_PER_GROUP].T.astype(jnp.float32)


def multiscale_pool(u, prev, pos0):
    B, T, P = u.shape
    ext = jnp.concatenate([prev.astype(jnp.float32), u.astype(jnp.float32)], axis=1)
    cs = jnp.concatenate([jnp.zeros((B, 1, P), jnp.float32), jnp.cumsum(ext, axis=1)], axis=1)
    pos = pos0 + jnp.arange(T, dtype=jnp.int32)
    R = POOL_STATE
    means = []
    for g, w in enumerate(POOL_WINDOWS):
        c0, c1 = g * POOL_GROUP, (g + 1) * POOL_GROUP
        hi = cs[:, R + 1:R + 1 + T, c0:c1]
        lo = cs[:, R + 1 - w:R + 1 - w + T, c0:c1]
        cnt = jnp.minimum(pos + 1, w).astype(jnp.float32)[None, :, None]
        means.append((hi - lo) / cnt)
    mean = jnp.concatenate(means, axis=-1)
    return (mean - u.astype(jnp.float32)).astype(u.dtype)


def band_attn_prompt(q, k, v, dilation, reach, bias):
    B, T, H, E = q.shape
    nb = -(-T // (dilation * BLK))
    Tp = nb * BLK * dilation

    def blocks(a):
        a = jnp.pad(a, ((0, 0), (0, Tp - T), (0, 0), (0, 0)))
        return a.reshape(B, nb, BLK, dilation, H, E)

    qb, kb, vb = blocks(q), blocks(k), blocks(v)
    zero = jnp.zeros_like(kb[:, :1])
    kw = jnp.concatenate([jnp.concatenate([zero, kb[:, :-1]], axis=1), kb], axis=2)
    vw = jnp.concatenate([jnp.concatenate([zero, vb[:, :-1]], axis=1), vb], axis=2)
    s = jnp.einsum('bnirhe,bnjrhe->bnrhij', qb, kw).astype(jnp.float32)
    i = jnp.arange(BLK)[:, None]
    j = jnp.arange(2 * BLK)[None, :]
    rel = i + BLK - j
    in_band = (rel >= 0) & (rel <= reach)
    has_prev = (jnp.arange(nb)[:, None, None] > 0) | (j >= BLK)[None]
    mask = in_band[None] & has_prev
    bias_ij = bias[:, jnp.clip(rel, 0, reach)]
    logits = jnp.where(mask[None, :, None, None], s + bias_ij[None, None, None], -jnp.inf)
    m = jnp.max(logits, axis=-1, keepdims=True)
    p = jnp.exp(logits - m)
    den = jnp.sum(p, axis=-1)
    o = jnp.einsum('bnrhij,bnjrhe->bnirhe', p, vw.astype(jnp.float32))
    o = o / jnp.transpose(den, (0, 1, 4, 2, 3))[..., None]
    lse = jnp.transpose(m[..., 0] + jnp.log(den), (0, 1, 4, 2, 3))
    return o.reshape(B, Tp, H, E)[:, :T], lse.reshape(B, Tp, H)[:, :T]


def band_attn_sample(q, k, v, kv_buf, pos0, dilation, reach, bias):
    B, S, H, E = q.shape
    L = kv_buf.shape[1]
    kv_new = jnp.stack([k, v], axis=2)
    ext = jnp.concatenate([kv_buf.astype(kv_new.dtype), kv_new], axis=1)
    i = jnp.arange(S, dtype=jnp.int32)[:, None]
    kk = jnp.arange(reach + 1, dtype=jnp.int32)[None, :]
    idx = L + i - kk * dilation
    valid = (pos0 + i - kk * dilation) >= 0
    g = ext[:, jnp.clip(idx, 0, L + S - 1)]
    s = jnp.einsum('bshe,bskhe->bhsk', q, g[:, :, :, 0]).astype(jnp.float32) + bias[None, :, None, :]
    logits = jnp.where(valid[None, None], s, -jnp.inf)
    m = jnp.max(logits, axis=-1, keepdims=True)
    p = jnp.exp(logits - m)
    den = jnp.sum(p, axis=-1)
    o = jnp.einsum('bhsk,bskhe->bshe', p, g[:, :, :, 1].astype(jnp.float32))
    o = o / jnp.transpose(den, (0, 2, 1))[..., None]
    lse = jnp.transpose(m[..., 0] + jnp.log(den), (0, 2, 1))
    return o, lse, ext[:, S:]


def token_mixer(h, pos0, pool_prev, bufs, biases, w_in, w_pool, pool_scale, q_norm, k_norm, w_pa, w_pb, w_o):
    B, T, _ = h.shape
    z = h @ w_in
    cuts = [POOL_WIDTH, POOL_WIDTH + ATTN_WIDTH, POOL_WIDTH + 2 * ATTN_WIDTH,
            POOL_WIDTH + 3 * ATTN_WIDTH, POOL_WIDTH + 3 * ATTN_WIDTH + D_MODEL]
    u, q, k, v, ga, gb = jnp.split(z, cuts, axis=-1)
    pooled = multiscale_pool(u, pool_prev, pos0)
    a = jnp.einsum('btgc,gcd->btgd', pooled.reshape(B, T, len(POOL_WINDOWS), POOL_GROUP), w_pool)
    a = a.reshape(B, T, POOL_WIDTH) * pool_scale
    new_pool = jnp.concatenate([pool_prev.astype(u.dtype), u], axis=1)[:, -POOL_STATE:]
    q = rmsnorm(q.reshape(B, T, N_HEADS, HEAD_DIM), q_norm) * (HEAD_DIM ** -0.5)
    k = rmsnorm(k.reshape(B, T, N_HEADS, HEAD_DIM), k_norm)
    v = v.reshape(B, T, N_HEADS, HEAD_DIM)
    outs, lses, new_bufs = [], [], []
    for g, (w, d) in enumerate(ATTN_GROUPS):
        hs = slice(g * HEADS_PER_GROUP, (g + 1) * HEADS_PER_GROUP)
        if bufs is None:
            o, lse = band_attn_prompt(q[:, :, hs], k[:, :, hs], v[:, :, hs], d, w // d, biases[g])
            keep = min(w, T)
            nbuf = jnp.stack([k[:, :, hs], v[:, :, hs]], axis=2)[:, T - keep:]
        else:
            o, lse, nbuf = band_attn_sample(q[:, :, hs], k[:, :, hs], v[:, :, hs], bufs[g], pos0, d, w // d, biases[g])
        outs.append(o)
        lses.append(lse)
        new_bufs.append(nbuf)
    wts = jax.nn.softmax(jnp.stack(lses, axis=0), axis=0)
    o = jnp.sum(wts[..., None] * jnp.stack(outs, axis=0), axis=0).reshape(B, T, GROUP_WIDTH).astype(h.dtype)
    merged = jax.nn.sigmoid(ga) * (a @ w_pa) + jax.nn.sigmoid(gb) * (o @ w_pb)
    return merged @ w_o, new_pool, new_bufs


def run_trunk(x, pos0, state_pool, caches, biases, w_norm1, w_in, w_pool, pool_scale, q_norm, k_norm,
              w_pa, w_pb, w_o, w_norm2, w_ff1, w_ff2):
    B = x.shape[0]
    new_pool, new_kv = [], [[] for _ in ATTN_GROUPS]
    for l in range(DEPTH):
        prev = jnp.zeros((B, POOL_STATE, POOL_WIDTH), x.dtype) if state_pool is None else state_pool[l]
        bufs = None if caches is None else [c[l] for c in caches]
        h = rmsnorm(x, w_norm1[l])
        mix, pool_rows, kv_rows = token_mixer(h, pos0, prev, bufs, biases, w_in[l], w_pool[l], pool_scale[l],
                                              q_norm[l], k_norm[l], w_pa[l], w_pb[l], w_o[l])
        x = x + mix
        h = rmsnorm(x, w_norm2[l])
        x = x + jnp.square(jax.nn.relu(h @ w_ff1[l])) @ w_ff2[l]
        new_pool.append(pool_rows)
        for g in range(N_ATTN_GROUPS):
            new_kv[g].append(kv_rows[g])
    return x, jnp.stack(new_pool, axis=0), [jnp.stack(r, axis=0) for r in new_kv]


def setup_inputs(seed: int = 0) -> dict:
    key = jax.random.key(seed)
    ks = jax.random.split(key, 20)
    f32 = jnp.float32

    def nrm(k, shape, s):
        return jax.random.normal(k, shape, f32) * s

    rows = [min(w, PAST_LEN) for w, _ in ATTN_GROUPS]
    kvshape = lambda r: (DEPTH, DEC_BATCH, r, 2, HEADS_PER_GROUP, HEAD_DIM)
    return {
        'x_prompt': nrm(ks[0], (BATCH, SEQ, D_MODEL), 1.0),
        'x_sample': nrm(ks[1], (DEC_BATCH, DEC_SEQ, D_MODEL), 1.0),
        'state_pool': nrm(ks[2], (DEPTH, DEC_BATCH, POOL_STATE, POOL_WIDTH), 1.0),
        'cache_kv_w128': nrm(ks[3], kvshape(rows[0]), 1.0),
        'cache_kv_w512': nrm(ks[4], kvshape(rows[1]), 1.0),
        'cache_kv_w2048': nrm(ks[5], kvshape(rows[2]), 1.0),
        'w_norm1': 1.0 + nrm(ks[6], (DEPTH, D_MODEL), 0.05),
        'w_in': nrm(ks[7], (DEPTH, D_MODEL, IN_WIDTH), D_MODEL ** -0.5),
        'w_pool': nrm(ks[8], (DEPTH, len(POOL_WINDOWS), POOL_GROUP, POOL_GROUP), POOL_GROUP ** -0.5),
        'pool_scale': 1.0 + nrm(ks[9], (DEPTH, POOL_WIDTH), 0.1),
        'q_norm': 1.0 + nrm(ks[10], (DEPTH, HEAD_DIM), 0.05),
        'k_norm': 1.0 + nrm(ks[11], (DEPTH, HEAD_DIM), 0.05),
        'rel_bias': nrm(ks[12], (N_BUCKETS, N_HEADS), 0.5),
        'w_pa': nrm(ks[13], (DEPTH, POOL_WIDTH, D_MODEL), POOL_WIDTH ** -0.5),
        'w_pb': nrm(ks[14], (DEPTH, GROUP_WIDTH, D_MODEL), GROUP_WIDTH ** -0.5),
        'w_o': nrm(ks[15], (DEPTH, D_MODEL, D_MODEL), D_MODEL ** -0.5),
        'w_norm2': 1.0 + nrm(ks[16], (DEPTH, D_MODEL), 0.05),
        'w_ff1': nrm(ks[17], (DEPTH, D_MODEL, D_FF), D_MODEL ** -0.5),
        'w_ff2': nrm(ks[18], (DEPTH, D_FF, D_MODEL), 0.5 * D_FF ** -0.5),
    }


def reference(x_prompt, x_sample, state_pool, cache_kv_w128, cache_kv_w512, cache_kv_w2048,
              w_norm1, w_in, w_pool, pool_scale, q_norm, k_norm, rel_bias, w_pa, w_pb, w_o,
              w_norm2, w_ff1, w_ff2):
    biases = [group_bias(rel_bias, g) for g in range(N_ATTN_GROUPS)]
    y_prompt, pool_p, kv_p = run_trunk(x_prompt, 0, None, None, biases, w_norm1, w_in, w_pool, pool_scale,
                                       q_norm, k_norm, w_pa, w_pb, w_o, w_norm2, w_ff1, w_ff2)
    y_sample, pool_s, kv_s = run_trunk(x_sample, PAST_LEN, state_pool,
                                       (cache_kv_w128, cache_kv_w512, cache_kv_w2048), biases,
                                       w_norm1, w_in, w_pool, pool_scale, q_norm, k_norm,
                                       w_pa, w_pb, w_o, w_norm2, w_ff1, w_ff2)
    return (y_prompt, y_sample, pool_p, kv_p[0], kv_p[1], kv_p[2], pool_s, kv_s[0], kv_s[1], kv_s[2])
```

```python
import math
from contextlib import ExitStack
import numpy as np
import concourse.bass as bass
import concourse.mybir as mybir
from concourse.bass_utils import run_bass_kernel_spmd

F32, BF16 = mybir.dt.float32, mybir.dt.bfloat16
ALU = mybir.AluOpType
AF = mybir.ActivationFunctionType
AX = mybir.AxisListType

D = 4096
NCH = 32
SEQ = 2048
HALF = 1024
NS = 8
TA = HALF + NS
XW = 2 * HALF + NS
DEPTH = 2
GROUPS = ((128, 1), (512, 4), (2048, 16))
EPS = 1e-6
NWSLOT = 3
PROMPT_CORES = [0, 1, 4, 5]
DBG = {}


def t5_bucket_np(dist):
    dist = np.asarray(dist, np.int64)
    nf = np.maximum(dist, 1).astype(np.float32)
    large = 16 + (np.log(nf / np.float32(16)) / np.float32(math.log(2048 / 16)) * np.float32(16)).astype(np.int32)
    large = np.minimum(large, 31)
    return np.where(dist < 16, dist, large).astype(np.int64)


def tile_w(W, k0, KC, n0):
    blk = W[k0:k0 + KC * 128, n0:n0 + 128]
    return np.ascontiguousarray(blk.reshape(KC, 128, 128).transpose(1, 0, 2)).reshape(-1)


def layer_tiles():
    tl = []
    for c in range(8):
        tl.append(("u", c, 32))
    for oc in range(8):
        tl.append(("pool", oc, 2))
    for j in range(8):
        for g in range(3):
            for which in range(3):
                tl.append(("qkv", (g, j, which), 32))
    for mg in range(2):
        for c in range(mg * 16, mg * 16 + 16):
            tl.append(("ga", c, 32))
            tl.append(("pa", c, 8))
            tl.append(("gb", c, 32))
            tl.append(("pb", c, 8))
        for c2 in range(32):
            tl.append(("wo", (mg, c2), 16))
    for hg in range(4):
        for jj in range(32):
            tl.append(("ff1", hg * 32 + jj, 32))
        for c2 in range(32):
            tl.append(("ff2", (hg, c2), 32))
    return tl


TILES = layer_tiles()
TOFF = []
_o = 0
for _k, _a, _kc in TILES:
    TOFF.append(_o)
    _o += 128 * _kc * 128
WLEN = _o


def pack_layer(l, w_in, w_pool, w_pa, w_pb, w_o, w_ff1, w_ff2):
    out = np.empty(WLEN, np.float32)
    Win, Wpa, Wpb, Wo, W1, W2 = w_in[l], w_pa[l], w_pb[l], w_o[l], w_ff1[l], w_ff2[l]
    for (kind, a, kc), off in zip(TILES, TOFF):
        n = 128 * kc * 128
        if kind == "u":
            t = tile_w(Win, 0, 32, a * 128)
        elif kind == "pool":
            g = a // 2
            t = tile_w(w_pool[l, g], 0, 2, (a % 2) * 128)
        elif kind == "qkv":
            g, j, which = a
            t = tile_w(Win, 0, 32, 1024 + which * 3072 + (g * 8 + j) * 128)
        elif kind == "ga":
            t = tile_w(Win, 0, 32, 10240 + a * 128)
        elif kind == "gb":
            t = tile_w(Win, 0, 32, 14336 + a * 128)
        elif kind == "pa":
            t = tile_w(Wpa, 0, 8, a * 128)
        elif kind == "pb":
            t = tile_w(Wpb, 0, 8, a * 128)
        elif kind == "wo":
            t = tile_w(Wo, a[0] * 2048, 16, a[1] * 128)
        elif kind == "ff1":
            t = tile_w(W1, 0, 32, a * 128)
        elif kind == "ff2":
            t = tile_w(W2, a[0] * 4096, 32, a[1] * 128)
        out[off:off + n] = t
    return out


def static_consts():
    c32 = np.zeros((32, 3 * 384 + 3 * 128 + 8), np.float32)
    for g, (w, d) in enumerate(GROUPS):
        for c in range(127, 256):
            c32[t5_bucket_np((c - 127) * d), g * 384 + c] = 1.0
        for m in range(128):
            c32[t5_bucket_np((128 - m) * d), 1152 + g * 128 + m] = 1.0
    c32[0, 1536:1544] = 1.0
    c128 = np.zeros((128, 384 + 128 + 64 + 64), np.float32)
    c128[:, 127:256] = 1.0
    c128[:, 384:512] = np.eye(128, dtype=np.float32)
    for i in range(8):
        c128[:, 512 + i * 8 + i] = 1.0
    for wi, w in enumerate((2, 4, 8, 16)):
        for t in range(16):
            c128[:, 576 + wi * 16 + t] = 1.0 / min(t + 1, w)
    c8 = np.zeros((8, 8, 128), np.float32)
    for i in range(8):
        c8[i, i, :] = 1.0
    return c32, c128, c8.reshape(8, 1024)


class Buf:
    __slots__ = ("lw", "rd")

    def __init__(self):
        self.lw = None
        self.rd = {}


class Prog:
    def __init__(self, nc):
        self.nc = nc
        self.engs = {"pe": nc.tensor, "act": nc.scalar, "dve": nc.vector, "pool": nc.gpsimd, "sp": nc.sync}
        self.esem = {k: nc.alloc_semaphore(name=f"e_{k}") for k in ("pe", "act", "dve")}
        self.ecnt = {k: 0 for k in self.esem}
        self.seen = {k: {} for k in self.engs}
        self.dsems = {"sp": [nc.alloc_semaphore(name=f"dsp{i}") for i in range(20)],
                      "pool": [nc.alloc_semaphore(name=f"dpl{i}") for i in range(6)]}
        self.dval = {}
        self.dnext = {"sp": 0, "pool": 0}
        self.dead = False

    def _deps(self, reads, writes):
        deps = {}

        def add(t):
            if t is not None and deps.get(t[0], 0) < t[1]:
                deps[t[0]] = t[1]
        for b in reads:
            add(b.lw)
        for b in writes:
            add(b.lw)
            for s, v in b.rd.items():
                add((s, v))
        return deps

    def _wait(self, e, deps):
        if self.dead:
            return
        eng = self.engs[e]
        seen = self.seen[e]
        for sem, val in deps.items():
            if e == "pe" and sem is self.esem["pe"]:
                continue
            if seen.get(sem, 0) >= val:
                continue
            eng.wait_ge(sem, val)
            seen[sem] = val

    def _mark(self, t, reads, writes):
        for b in reads:
            if b.rd.get(t[0], 0) < t[1]:
                b.rd[t[0]] = t[1]
        for b in writes:
            b.lw = t
            b.rd = {}

    def op(self, e, fn, reads=(), writes=(), signal=True):
        if self.dead:
            return None
        self._wait(e, self._deps(reads, writes))
        ins = fn(self.engs[e])
        sem = self.esem[e]
        if signal:
            self.ecnt[e] += 1
            ins.then_inc(sem, 1)
            t = (sem, self.ecnt[e])
        else:
            t = (sem, self.ecnt[e] + 1)
        self._mark(t, reads, writes)
        return t

    def dma(self, q, out, in_, reads=(), writes=(), **kw):
        if self.dead:
            return None
        self._wait(q, self._deps(reads, writes))
        sems = self.dsems[q]
        i = self.dnext[q]
        self.dnext[q] = (i + 1) % len(sems)
        sem = sems[i]
        prev = self.dval.get(sem, 0)
        if prev:
            self._wait(q, {sem: prev})
        ins = self.engs[q].dma_start(out=out, in_=in_, **kw)
        ins.then_inc(sem, 16)
        self.dval[sem] = prev + 16
        t = (sem, prev + 16)
        self._mark(t, reads, writes)
        return t

    def finish(self):
        for q in ("sp", "pool"):
            for sem in self.dsems[q]:
                v = self.dval.get(sem, 0)
                if v:
                    self._wait("sp", {sem: v})
        for k, sem in self.esem.items():
            if self.ecnt[k]:
                self._wait("sp", {sem: self.ecnt[k]})


class StopBuild(Exception):
    pass


def build_program(stop=None, small_w=False):
    nc = bass.Bass("TRN2", target_bir_lowering=False)
    P = Prog(nc)
    dumps = []

    def chk(tag, tensors):
        if stop != tag or P.dead:
            return
        for name, ap, bufs in tensors:
            dd = nc.dram_tensor("dbg_" + name, list(ap.shape), ap.dtype, kind="ExternalOutput")
            P.dma("sp", dd.ap(), ap, reads=bufs)
            dumps.append("dbg_" + name)
        P.finish()
        P.dead = True

    def din(name, shape, dt=F32):
        return nc.dram_tensor(name, list(shape), dt, kind="ExternalInput")

    def dout(name, shape, dt=F32):
        return nc.dram_tensor(name, list(shape), dt, kind="ExternalOutput")

    xT = din("xT", [NCH, 128, XW])
    wd = [din(f"w{l}", [16 if small_w else WLEN]) for l in range(DEPTH)]
    prm_d = din("prm", [128, 148])
    relb_d = din("relb", [32, 24])
    c32_d = din("c32", [32, 1544])
    c128_d = din("c128", [128, 640])
    c8_d = din("c8", [8, 1024])
    spool_d = din("spool", [DEPTH, 8, 128, 15])
    ck_d = [din(f"ck{g}", [DEPTH, GROUPS[g][0], 2048]) for g in range(3)]

    yT = dout("yT", [NCH, 128, XW])
    poolp_d = dout("poolp", [DEPTH, 8, 128, 15])
    pools_d = dout("pools", [DEPTH, 8, 128, 15])
    kvo_d = [dout(f"kvo{g}", [DEPTH, 2, 8, 128, GROUPS[g][0]]) for g in range(3)]
    sk_d = [dout(f"sk{g}", [DEPTH, GROUPS[g][0], 2048]) for g in range(3)]

    kvprev_d = nc.dram_tensor("kvprev", [24, 2, 128, HALF], BF16, kind="Internal")
    rtab_d = nc.dram_tensor("rtab", [24, 128, 384], F32, kind="Internal")
    snew_d = nc.dram_tensor("snew", [8, 3, 24, 128], F32, kind="Internal")

    xb = [Buf() for _ in range(NCH)]
    kvprev_b = [Buf() for _ in range(24)]
    snew_b = Buf()
    skout_b = [Buf() for _ in range(3)]
    misc_out = Buf()

    st = ExitStack()

    uniq = [0]

    def sb(name, shape, dt, stack=None):
        uniq[0] += 1
        return (stack or st).enter_context(nc.sbuf_tensor(f"{name}_{uniq[0]}", list(shape), dt))

    def pst_(name, shape, dt, stack=None):
        return (stack or st).enter_context(nc.psum_tensor(name, list(shape), dt))

    hT = sb("hT", [128, NCH, TA], BF16)
    hT_b = [Buf() for _ in range(NCH)]
    wbuf = sb("wbuf", [128, NWSLOT, 32 * 128], BF16)
    wbuf_b = [Buf() for _ in range(NWSLOT)]
    tcur = sb("tcur", [128, 24, 128], BF16)
    tprev = sb("tprev", [128, 24, 128], BF16)
    tab_b = Buf()
    c128 = sb("c128s", [128, 640], F32)
    prm = sb("prms", [128, 148], F32)
    relb = sb("relbs", [32, 24], F32)
    cst_b = Buf()
    identb = sb("identb", [128, 128], BF16)
    onesb = sb("onesb", [128, 128], BF16)
    biasT = sb("biasT", [128, 3, 8], F32)
    b0 = sb("b0", [8, 24], F32)
    qgs = sb("qgs", [128, 2], F32)
    uprev = sb("uprev", [128, 8, 15], F32)
    uprev_b = Buf()

    ps = [pst_(f"ps{i}", [128, 512], F32) for i in range(7)]
    ps_b = [Buf() for _ in range(7)]
    pst = pst_("pst", [128, 1024], BF16)
    pst_b = Buf()
    PA, PB = [0, 1, 2], [3, 4, 5]
    PC = 6

    ident = c128[:, 384:512]
    maskx = c128[:, 0:384]

    def selT(i):
        return c128[:, 512 + i * 8: 512 + i * 8 + 8]

    wseq = [(l, ti) for l in range(DEPTH) for hf in range(2) for ti in range(len(TILES))]
    wstate = {"issued": 0, "next": 0}

    def w_issue_upto(n):
        while wstate["issued"] < min(n, len(wseq)):
            k = wstate["issued"]
            l, ti = wseq[k]
            kc = TILES[ti][2]
            slot = k % NWSLOT
            src = bass.AP(wd[l], TOFF[ti], [[kc * 128, 128], [1, kc * 128]])
            P.dma("pool", wbuf[:, slot, 0:kc * 128], src, writes=[wbuf_b[slot]])
            wstate["issued"] += 1

    class WT:
        pass

    def next_w(expect):
        k = wstate["next"]
        l, ti = wseq[k]
        assert TILES[ti][0] == expect, (TILES[ti], expect)
        w_issue_upto(k + NWSLOT)
        wstate["next"] += 1
        r = WT()
        r.slot = k % NWSLOT
        r.buf = wbuf_b[r.slot]
        r.kc = TILES[ti][2]
        return r

    def gemm(pbanks, wt, acts, cts):
        KC = wt.kc
        for ci, (c0, c1) in enumerate(cts):
            pi = pbanks[ci]
            for kc in range(KC):
                ap, b = acts[kc]
                P.op("pe", lambda e, pi=pi, kc=kc, ap=ap, c0=c0, c1=c1: e.matmul(
                    ps[pi][:, 0:c1 - c0], lhsT=wbuf[:, wt.slot, kc * 128:(kc + 1) * 128], rhs=ap[:, c0:c1],
                    start=(kc == 0), stop=(kc == KC - 1)),
                    reads=[wt.buf, b], writes=[ps_b[pi]], signal=(kc == KC - 1))

    for dst, src in ((c128, c128_d), (prm, prm_d), (relb, relb_d)):
        P.dma("sp", dst[:], src.ap(), writes=[cst_b])
    P.op("dve", lambda e: e.tensor_copy(out=identb[:], in_=ident), reads=[cst_b], writes=[cst_b])
    P.op("dve", lambda e: e.memset(onesb[:], 1.0), writes=[cst_b])
    P.op("dve", lambda e: e.tensor_scalar(out=qgs[:], in0=prm[:, 144:146], scalar1=float(128 ** -0.5), scalar2=None,
                                          op0=ALU.mult), reads=[cst_b], writes=[cst_b])
    for c in range(NCH):
        P.dma("sp", yT[c], xT[c], writes=[xb[c]])
    for l in range(DEPTH):
        for g in range(3):
            L = GROUPS[g][0]
            P.dma("sp", sk_d[g][l, 0:L - 8, :], ck_d[g][l, 8:L, :])

    with ExitStack() as s1:
        c32 = sb("c32s", [32, 1544], F32, s1)
        c32_b = Buf()
        P.dma("sp", c32[:], c32_d.ap(), writes=[c32_b])
        rb_bc = sb("rb_bc", [32, 24, 128], F32, s1)
        ebx = sb("ebx", [128, 2, 384], F32, s1)
        rb_b = Buf()
        ebx_b = [Buf(), Buf()]
        rt_b = [Buf() for _ in range(24)]
        P.op("dve", lambda e: e.tensor_copy(out=rb_bc[:], in_=relb[:].unsqueeze(2).to_broadcast([32, 24, 128])),
             reads=[cst_b], writes=[rb_b])
        for hh in range(24):
            g = hh // 8
            k = hh % 2
            P.op("pe", lambda e, hh=hh, g=g, k=k: e.matmul(ps[k][:, 0:384], lhsT=rb_bc[:, hh, :],
                                                          rhs=c32[:, g * 384:(g + 1) * 384], start=True, stop=True),
                 reads=[rb_b, c32_b], writes=[ps_b[k]])
            P.op("act", lambda e, k=k: e.activation(out=ebx[:, k, :], in_=ps[k][:, 0:384], func=AF.Exp),
                 reads=[ps_b[k]], writes=[ebx_b[k]])
            P.op("dve", lambda e, k=k: e.tensor_tensor(out=ebx[:, k, :], in0=ebx[:, k, :], in1=maskx, op=ALU.mult),
                 reads=[ebx_b[k], cst_b], writes=[ebx_b[k]])
            P.dma("sp", rtab_d[hh], ebx[:, k, :], reads=[ebx_b[k]], writes=[rt_b[hh]])
            base = hh * 128 * 384
            P.dma("pool", tcur[:, hh, :], bass.AP(rtab_d, base + 127, [[383, 128], [1, 128]]),
                  reads=[rt_b[hh]], writes=[tab_b])
            P.dma("pool", tprev[:, hh, :], bass.AP(rtab_d, base + 255, [[383, 128], [1, 128]]),
                  reads=[rt_b[hh]], writes=[tab_b])
        for g in range(3):
            P.op("pe", lambda e, g=g: e.matmul(ps[2][:, 0:8], lhsT=c32[:, 1152 + g * 128:1152 + (g + 1) * 128],
                                               rhs=relb[:, g * 8:(g + 1) * 8], start=True, stop=True),
                 reads=[cst_b, c32_b], writes=[ps_b[2]])
            P.op("act", lambda e, g=g: e.copy(out=biasT[:, g, :], in_=ps[2][:, 0:8]), reads=[ps_b[2]], writes=[tab_b])
        P.op("pe", lambda e: e.matmul(ps[3][0:8, 0:24], lhsT=c32[:, 1536:1544], rhs=relb[:], start=True, stop=True),
             reads=[cst_b, c32_b], writes=[ps_b[3]])
        P.op("act", lambda e: e.copy(out=b0[:], in_=ps[3][0:8, 0:24]), reads=[ps_b[3]], writes=[tab_b])
        scope_bufs = [rb_b, c32_b] + ebx_b
        barrier_bufs = list(scope_bufs)

    pending_barrier = [barrier_bufs]

    def scope_fence(bufs):
        deps = {}
        for b in bufs:
            for t in ([b.lw] if b.lw else []) + list(b.rd.items()):
                if deps.get(t[0], 0) < t[1]:
                    deps[t[0]] = t[1]
        for e in ("pe", "act", "dve", "sp", "pool"):
            P._wait(e, deps)

    def xs(c, off, T):
        return yT[c][:, off:off + T]

    def rmsnorm(l, which, off, T, cts, stack):
        xt = sb("n_xt", [128, 2, TA], F32, stack)
        sq = sb("n_sq", [128, 2, TA], BF16, stack)
        rstd = sb("n_rstd", [128, TA], F32, stack)
        rstd_b = Buf()
        xt_b = [Buf(), Buf()]
        sq_b = [Buf(), Buf()]
        for c in range(NCH):
            k = c % 2
            P.dma("sp", xt[:, k, 0:T], xs(c, off, T), reads=[xb[c]], writes=[xt_b[k]])
            P.op("act", lambda e, k=k: e.activation(out=sq[:, k, 0:T], in_=xt[:, k, 0:T], func=AF.Square),
                 reads=[xt_b[k]], writes=[sq_b[k]])
            chk("n1a", [("xt", xt[:], xt_b), ("sq", sq[:], sq_b)])
            for ci, (c0, c1) in enumerate(cts):
                P.op("pe", lambda e, ci=ci, k=k, c0=c0, c1=c1, c=c: e.matmul(
                    ps[PA[ci]][:, 0:c1 - c0], lhsT=onesb[:], rhs=sq[:, k, c0:c1], start=(c == 0), stop=(c == NCH - 1)),
                    reads=[sq_b[k], cst_b], writes=[ps_b[PA[ci]]], signal=(c == NCH - 1 or ci == len(cts) - 1))
            if c == 1:
                chk("n1a2", [("xt", xt[:], xt_b), ("sq", sq[:], sq_b)])
        chk("n1b", [("sq", sq[:], sq_b)])
        for ci, (c0, c1) in enumerate(cts):
            P.op("act", lambda e, ci=ci, c0=c0, c1=c1: e.activation(out=rstd[:, c0:c1], in_=ps[PA[ci]][:, 0:c1 - c0],
                                                                   func=AF.Sqrt, bias=epsb[:, 0:1], scale=1.0 / D),
                 reads=[ps_b[PA[ci]], cst_b], writes=[rstd_b])
        chk("n1c", [("rstd", rstd[:], [rstd_b])])
        P.op("dve", lambda e: e.reciprocal(out=rstd[:, 0:T], in_=rstd[:, 0:T]), reads=[rstd_b], writes=[rstd_b])
        chk("n1d", [("rstd", rstd[:], [rstd_b])])
        gcol = l * 64 + which * 32
        for c in range(NCH):
            k = c % 2
            P.dma("sp", xt[:, k, 0:T], xs(c, off, T), reads=[xb[c]], writes=[xt_b[k]])
            P.op("dve", lambda e, k=k, c=c: e.scalar_tensor_tensor(
                out=hT[:, c, 0:T], in0=xt[:, k, 0:T], scalar=prm[:, gcol + c:gcol + c + 1], in1=rstd[:, 0:T],
                op0=ALU.mult, op1=ALU.mult), reads=[xt_b[k], rstd_b, cst_b], writes=[hT_b[c]])
        return [xt_b[0], xt_b[1], sq_b[0], sq_b[1], rstd_b]

    epsb = sb("epsb", [128, 1], F32)
    P.op("dve", lambda e: e.memset(epsb[:], EPS), writes=[cst_b])

    hacts = lambda T: [(hT[:, c, :], hT_b[c]) for c in range(NCH)]

    def residual_loop(n, T, off, cts, produce, stack, tag):
        xt = sb(f"r_xt{tag}", [128, 2, TA], F32, stack)
        xo = sb(f"r_xo{tag}", [128, 2, TA], F32, stack)
        xt_b = [Buf(), Buf()]
        xo_b = [Buf(), Buf()]
        P.dma("sp", xt[:, 0, 0:T], xs(0, off, T), reads=[xb[0]], writes=[xt_b[0]])
        for c2 in range(n):
            k = c2 % 2
            pb = PA if k == 0 else PB
            produce(c2, pb)
            if c2 + 1 < n:
                P.dma("sp", xt[:, 1 - k, 0:T], xs(c2 + 1, off, T), reads=[xb[c2 + 1]], writes=[xt_b[1 - k]])
            for ci, (c0, c1) in enumerate(cts):
                P.op("dve", lambda e, ci=ci, c0=c0, c1=c1, k=k, pb=pb: e.tensor_tensor(
                    out=xo[:, k, c0:c1], in0=xt[:, k, c0:c1], in1=ps[pb[ci]][:, 0:c1 - c0], op=ALU.add),
                    reads=[xt_b[k], ps_b[pb[ci]]], writes=[xo_b[k]])
            P.dma("sp", xs(c2, off, T), xo[:, k, 0:T], reads=[xo_b[k]], writes=[xb[c2]])
        return xt_b + xo_b

    def layer_half(l, hf):
        T = TA if hf == 0 else HALF
        off = 0 if hf == 0 else TA
        cts = [(0, 344), (344, 688), (688, 1032)] if hf == 0 else [(0, 512), (512, 1024)]
        fence = []

        with ExitStack() as mix:
            scope_fence(pending_barrier.pop())
            aT = sb("aT", [128, 8, TA], BF16, mix)
            oT = sb("oT", [128, 8, TA], BF16, mix)
            aT_b = [Buf() for _ in range(8)]
            oT_b = [Buf() for _ in range(8)]
            mixfence = aT_b + oT_b

            with ExitStack() as s:
                fb = rmsnorm(l, 0, off, T, cts, s)
            scope_fence(fb)
            chk(f"norm1_{l}{hf}", [("hT", hT[:], hT_b)])

            with ExitStack() as s:
                E = sb("p_E", [128, 15 + HALF], F32, s)
                Es = sb("p_Es", [128, 32], F32, s)
                Sa = sb("p_Sa", [128, 15 + HALF], F32, s)
                Sb = sb("p_Sb", [128, 15 + HALF], F32, s)
                t16 = sb("p_t16", [128, 16], F32, s)
                pooled = sb("p_pool", [128, 8, TA], BF16, s)
                E_b, Es_b, Sa_b, Sb_b, t16_b = Buf(), Buf(), Buf(), Buf(), Buf()
                pooled_b = [Buf() for _ in range(8)]
                for c in range(8):
                    wt = next_w("u")
                    pbk = PA if c % 2 == 0 else PB
                    gemm(pbk, wt, hacts(T), cts)
                    if hf == 0:
                        P.op("dve", lambda e: e.memset(E[:, 0:15], 0.0), writes=[E_b])
                        P.dma("sp", Es[:, 0:15], spool_d[l, c], writes=[Es_b])
                    else:
                        P.op("dve", lambda e, c=c: e.tensor_copy(out=E[:, 0:15], in_=uprev[:, c, :]),
                             reads=[uprev_b], writes=[E_b])
                    for ci, (c0, c1) in enumerate(cts):
                        cp = min(c1, HALF)
                        P.op("act", lambda e, ci=ci, c0=c0, cp=cp, pbk=pbk: e.copy(
                            out=E[:, 15 + c0:15 + cp], in_=ps[pbk[ci]][:, 0:cp - c0]),
                            reads=[ps_b[pbk[ci]]], writes=[E_b])
                        if c1 > HALF:
                            P.op("act", lambda e, ci=ci, c0=c0, c1=c1, pbk=pbk: e.copy(
                                out=Es[:, 15:15 + NS], in_=ps[pbk[ci]][:, HALF - c0:c1 - c0]),
                                reads=[ps_b[pbk[ci]]], writes=[Es_b])
                    chk("p2a", [("E", E[:], [E_b]), ("Es", Es[:], [Es_b])])
                    gi = c // 2
                    w = (2, 4, 8, 16)[gi]

                    def winsum(src, src_b, N, tagb):
                        cur, cur_b = src, src_b
                        tmps = [(Sa, Sa_b), (Sb, Sb_b)]
                        for s_ in range(gi + 1):
                            sh = 1 << s_
                            lo = (1 << (s_ + 1)) - 1
                            dst, dst_b = tmps[s_ % 2]
                            P.op("dve", lambda e, cur=cur, dst=dst, sh=sh, lo=lo: e.tensor_tensor(
                                out=dst[:, lo:N], in0=cur[:, lo:N], in1=cur[:, lo - sh:N - sh], op=ALU.add),
                                reads=[cur_b], writes=[dst_b])
                            cur, cur_b = dst, dst_b
                        return cur, cur_b

                    S, S_b = winsum(E, E_b, 15 + HALF, "p")
                    P.op("dve", lambda e, S=S, c=c: e.scalar_tensor_tensor(
                        out=pooled[:, c, 0:HALF], in0=S[:, 15:15 + HALF], scalar=1.0 / w, in1=E[:, 15:15 + HALF],
                        op0=ALU.mult, op1=ALU.subtract), reads=[S_b, E_b], writes=[pooled_b[c]])
                    chk("p2b1", [("pooled", pooled[:, 0, :], [pooled_b[0]])])
                    if hf == 0:
                        P.op("dve", lambda e, S=S: e.tensor_tensor(out=t16[:], in0=S[:, 15:31],
                                                                   in1=c128[:, 576 + gi * 16:576 + gi * 16 + 16],
                                                                   op=ALU.mult),
                             reads=[S_b, cst_b], writes=[t16_b])
                        P.op("dve", lambda e, c=c: e.tensor_tensor(out=pooled[:, c, 0:16], in0=t16[:], in1=E[:, 15:31],
                                                                   op=ALU.subtract),
                             reads=[t16_b, E_b], writes=[pooled_b[c]])
                        P.op("dve", lambda e, c=c: e.tensor_copy(out=uprev[:, c, :], in_=E[:, HALF:HALF + 15]),
                             reads=[E_b], writes=[uprev_b])
                        chk("p2b2", [("pooled", pooled[:, 0, :], [pooled_b[0]])])
                        S2, S2_b = winsum(Es, Es_b, 15 + NS, "s")
                        P.op("dve", lambda e, S2=S2, c=c: e.scalar_tensor_tensor(
                            out=pooled[:, c, HALF:TA], in0=S2[:, 15:15 + NS], scalar=1.0 / w, in1=Es[:, 15:15 + NS],
                            op0=ALU.mult, op1=ALU.subtract), reads=[S2_b, Es_b], writes=[pooled_b[c]])
                        chk("p2b3", [("pooled", pooled[:, 0, :], [pooled_b[0]])])
                        P.dma("sp", pools_d[l, c], Es[:, 8:23], reads=[Es_b])
                        chk("p2b4", [("pooled", pooled[:, 0, :], [pooled_b[0]])])
                    else:
                        P.dma("sp", poolp_d[l, c], E[:, HALF:HALF + 15], reads=[E_b])
                    chk("p2b", [("E", E[:], [E_b]), ("pooled", pooled[:, 0, :], [pooled_b[0]])])
                chk("p2c", [("pooled", pooled[:], pooled_b)])
                for oc in range(8):
                    wt = next_w("pool")
                    pbk = PA if oc % 2 == 0 else PB
                    g2 = oc // 2
                    gemm(pbk, wt, [(pooled[:, 2 * g2 + kk, :], pooled_b[2 * g2 + kk]) for kk in range(2)], cts)
                    for ci, (c0, c1) in enumerate(cts):
                        P.op("act", lambda e, ci=ci, c0=c0, c1=c1, pbk=pbk, oc=oc: e.activation(
                            out=aT[:, oc, c0:c1], in_=ps[pbk[ci]][:, 0:c1 - c0], func=AF.Copy,
                            scale=prm[:, 128 + l * 8 + oc:128 + l * 8 + oc + 1]),
                            reads=[ps_b[pbk[ci]], cst_b], writes=[aT_b[oc]])
                fb = [E_b, Es_b, Sa_b, Sb_b, t16_b] + pooled_b
            scope_fence(fb)
            chk(f"pool_{l}{hf}", [("aT", aT[:], aT_b)])

            with ExitStack() as s:
                qn = [sb(f"a_qn{g}", [128, TA], BF16, s) for g in range(3)]
                kx = [sb(f"a_kx{g}", [128, GROUPS[g][0] // (2 if g == 2 else 1) + TA], BF16, s) for g in range(3)]
                vx = [sb(f"a_vx{g}", [128, GROUPS[g][0] // (2 if g == 2 else 1) + TA], BF16, s) for g in range(3)]
                qn_b = [Buf() for _ in range(3)]
                kx_b = [Buf() for _ in range(3)]
                vx_b = [Buf() for _ in range(3)]
                zs = sb("a_zs", [128, TA], F32, s)
                sq = sb("a_sq", [128, TA], BF16, s)
                rs = sb("a_rs", [128, TA], F32, s)
                nf = sb("a_nf", [128, TA], F32, s)
                zs_b, sq_b, rs_b, nf_b = Buf(), Buf(), Buf(), Buf()
                acc_o = sb("a_acco", [128, HALF], F32, s)
                acc_d = sb("a_accd", [128, HALF], F32, s)
                acc_b = Buf()
                vt = sb("a_vt", [128, 17, 128], BF16, s)
                vt_b = [Buf() for _ in range(17)]
                pt = sb("a_pt", [128, 4, 128], BF16, s)
                pt_b = [Buf() for _ in range(4)]
                stm = sb("a_stm", [8, 3, 128], F32, s)
                stm_b = Buf()
                fb = qn_b + kx_b + vx_b + [zs_b, sq_b, rs_b, nf_b, acc_b, stm_b] + vt_b + pt_b

                def qknorm(pbk, gaincol, out16, out16_b, sample_slot, hh):
                    for ci, (c0, c1) in enumerate(cts):
                        P.op("act", lambda e, ci=ci, c0=c0, c1=c1: e.copy(out=zs[:, c0:c1], in_=ps[pbk[ci]][:, 0:c1 - c0]),
                             reads=[ps_b[pbk[ci]]], writes=[zs_b])
                        P.op("act", lambda e, ci=ci, c0=c0, c1=c1: e.activation(out=sq[:, c0:c1],
                                                                               in_=ps[pbk[ci]][:, 0:c1 - c0],
                                                                               func=AF.Square),
                             reads=[ps_b[pbk[ci]]], writes=[sq_b])
                    for ci, (c0, c1) in enumerate(cts):
                        P.op("pe", lambda e, c0=c0, c1=c1: e.matmul(ps[PC][:, 0:c1 - c0], lhsT=onesb[:], rhs=sq[:, c0:c1],
                                                                   start=True, stop=True),
                             reads=[sq_b, cst_b], writes=[ps_b[PC]])
                        P.op("act", lambda e, c0=c0, c1=c1: e.activation(out=rs[:, c0:c1], in_=ps[PC][:, 0:c1 - c0],
                                                                        func=AF.Sqrt, bias=epsb[:, 0:1], scale=1.0 / 128),
                             reads=[ps_b[PC], cst_b], writes=[rs_b])
                    P.op("dve", lambda e: e.reciprocal(out=rs[:, 0:T], in_=rs[:, 0:T]), reads=[rs_b], writes=[rs_b])
                    P.op("dve", lambda e: e.scalar_tensor_tensor(out=nf[:, 0:T], in0=zs[:, 0:T], scalar=gaincol,
                                                                 in1=rs[:, 0:T], op0=ALU.mult, op1=ALU.mult),
                         reads=[zs_b, rs_b, cst_b], writes=[nf_b])
                    P.op("act", lambda e: e.copy(out=out16, in_=nf[:, 0:T]), reads=[nf_b], writes=[out16_b])
                    if hf == 0:
                        sample_tm(nf, nf_b, sample_slot, hh)

                def sample_tm(src, src_b, slot, hh):
                    P.op("pe", lambda e: e.transpose(out=ps[PC][0:8, 0:128], in_=src[:, HALF:TA], identity=ident),
                         reads=[src_b, cst_b], writes=[ps_b[PC]])
                    P.op("act", lambda e: e.copy(out=stm[:, slot, :], in_=ps[PC][0:8, 0:128]),
                         reads=[ps_b[PC]], writes=[stm_b])
                    P.dma("sp", snew_d[:, slot, hh, :], stm[:, slot, :], reads=[stm_b, snew_b])

                def attn_core(g, j):
                    hh = g * 8 + j
                    Lw, d = GROUPS[g]
                    Lp = 0 if hf == 0 else (Lw if g < 2 else HALF)
                    first = (g == 0)
                    sbank = [0, 1]
                    obank = [2, 3]
                    it = 0
                    if g < 2:
                        nblk = HALF // (128 * d)
                        blocks = list(range(-1 if Lp else 0, nblk))
                        vidx = {}
                        for r in range(d):
                            for b in blocks:
                                vi = len(vidx)
                                vidx[(r, b)] = vi
                                st0 = Lp + b * 128 * d + r
                                P.op("pe", lambda e, st0=st0, d=d: e.transpose(
                                    out=pst[:, 0:128], in_=vx[g][:, st0:st0 + 127 * d + 1:d], identity=identb[:]),
                                    reads=[vx_b[g], cst_b], writes=[pst_b])
                                P.op("dve", lambda e, vi=vi: e.tensor_copy(out=vt[:, vi, :], in_=pst[:, 0:128]),
                                     reads=[pst_b], writes=[vt_b[vi]])
                        for r in range(d):
                            for n in range(nblk):
                                sbk = sbank[it % 2]
                                obk = obank[it % 2]
                                it += 1
                                q0 = n * 128 * d + r
                                qap = qn[g][:, q0:q0 + 127 * d + 1:d]
                                kbs = [n] + ([n - 1] if (n - 1) in blocks else [])
                                pts = []
                                for ki, b in enumerate(kbs):
                                    st0 = Lp + b * 128 * d + r
                                    P.op("pe", lambda e, st0=st0, ki=ki, sbk=sbk, qap=qap, d=d: e.matmul(
                                        ps[sbk][:, ki * 128:(ki + 1) * 128],
                                        lhsT=kx[g][:, st0:st0 + 127 * d + 1:d], rhs=qap, start=True, stop=True),
                                        reads=[kx_b[g], qn_b[g]], writes=[ps_b[sbk]])
                                    pi = (it % 2) * 2 + ki
                                    tab = tcur if ki == 0 else tprev
                                    P.op("act", lambda e, pi=pi, ki=ki, sbk=sbk: e.activation(
                                        out=pt[:, pi, :], in_=ps[sbk][:, ki * 128:(ki + 1) * 128], func=AF.Exp),
                                        reads=[ps_b[sbk]], writes=[pt_b[pi]])
                                    P.op("dve", lambda e, pi=pi, tab=tab: e.tensor_tensor(
                                        out=pt[:, pi, :], in0=pt[:, pi, :], in1=tab[:, hh, :], op=ALU.mult),
                                        reads=[pt_b[pi], tab_b], writes=[pt_b[pi]])
                                    pts.append((pi, vidx[(r, b)]))
                                for k2, (pi, vi) in enumerate(pts):
                                    P.op("pe", lambda e, pi=pi, vi=vi, k2=k2, obk=obk, npt=len(pts): e.matmul(
                                        ps[obk][:, 0:128], lhsT=vt[:, vi, :], rhs=pt[:, pi, :],
                                        start=(k2 == 0), stop=(k2 == npt - 1)),
                                        reads=[vt_b[vi], pt_b[pi]], writes=[ps_b[obk]], signal=(k2 == len(pts) - 1))
                                for k2, (pi, vi) in enumerate(pts):
                                    P.op("pe", lambda e, pi=pi, k2=k2, obk=obk, npt=len(pts): e.matmul(
                                        ps[obk][:, 128:256], lhsT=onesb[:], rhs=pt[:, pi, :],
                                        start=(k2 == 0), stop=(k2 == npt - 1)),
                                        reads=[cst_b, pt_b[pi]], writes=[ps_b[obk]], signal=(k2 == len(pts) - 1))
                                oap = acc_o[:, q0:q0 + 127 * d + 1:d]
                                dap = acc_d[:, q0:q0 + 127 * d + 1:d]
                                if first:
                                    P.op("act", lambda e, oap=oap, obk=obk: e.copy(out=oap, in_=ps[obk][:, 0:128]),
                                         reads=[ps_b[obk]], writes=[acc_b])
                                    P.op("act", lambda e, dap=dap, obk=obk: e.copy(out=dap, in_=ps[obk][:, 128:256]),
                                         reads=[ps_b[obk]], writes=[acc_b])
                                else:
                                    P.op("dve", lambda e, oap=oap, obk=obk: e.tensor_tensor(
                                        out=oap, in0=oap, in1=ps[obk][:, 0:128], op=ALU.add),
                                        reads=[ps_b[obk]], writes=[acc_b])
                                    P.op("dve", lambda e, dap=dap, obk=obk: e.tensor_tensor(
                                        out=dap, in0=dap, in1=ps[obk][:, 128:256], op=ALU.add),
                                        reads=[ps_b[obk]], writes=[acc_b])
                    else:
                        nk = 64 + Lp // 16
                        toff = 0 if hf == 0 else 64
                        for r in range(16):
                            sbk = sbank[it % 2]
                            obk = obank[it % 2]
                            pi = it % 4
                            vi = it % 16
                            it += 1
                            kap = kx[g][:, r:r + (nk - 1) * 16 + 1:16]
                            vap = vx[g][:, r:r + (nk - 1) * 16 + 1:16]
                            qap = qn[g][:, r:r + 63 * 16 + 1:16]
                            P.op("pe", lambda e, vap=vap: e.transpose(out=pst[0:nk, 0:128], in_=vap, identity=identb[:]),
                                 reads=[vx_b[g], cst_b], writes=[pst_b])
                            P.op("dve", lambda e, vi=vi: e.tensor_copy(out=vt[0:nk, vi, :], in_=pst[0:nk, 0:128]),
                                 reads=[pst_b], writes=[vt_b[vi]])
                            P.op("pe", lambda e, kap=kap, qap=qap, sbk=sbk: e.matmul(
                                ps[sbk][0:nk, 0:64], lhsT=kap, rhs=qap, start=True, stop=True),
                                reads=[kx_b[g], qn_b[g]], writes=[ps_b[sbk]])
                            P.op("act", lambda e, pi=pi, sbk=sbk: e.activation(
                                out=pt[0:nk, pi, 0:64], in_=ps[sbk][0:nk, 0:64], func=AF.Exp),
                                reads=[ps_b[sbk]], writes=[pt_b[pi]])
                            P.op("dve", lambda e, pi=pi: e.tensor_tensor(
                                out=pt[0:nk, pi, 0:64], in0=pt[0:nk, pi, 0:64], in1=tcur[0:nk, hh, toff:toff + 64],
                                op=ALU.mult), reads=[pt_b[pi], tab_b], writes=[pt_b[pi]])
                            P.op("pe", lambda e, pi=pi, vi=vi, obk=obk: e.matmul(
                                ps[obk][:, 0:64], lhsT=vt[0:nk, vi, :], rhs=pt[0:nk, pi, 0:64], start=True, stop=True),
                                reads=[vt_b[vi], pt_b[pi]], writes=[ps_b[obk]])
                            P.op("pe", lambda e, pi=pi, obk=obk: e.matmul(
                                ps[obk][:, 128:192], lhsT=onesb[0:nk, :], rhs=pt[0:nk, pi, 0:64], start=True, stop=True),
                                reads=[cst_b, pt_b[pi]], writes=[ps_b[obk]])
                            oap = acc_o[:, r:r + 63 * 16 + 1:16]
                            dap = acc_d[:, r:r + 63 * 16 + 1:16]
                            P.op("dve", lambda e, oap=oap, obk=obk: e.tensor_tensor(
                                out=oap, in0=oap, in1=ps[obk][:, 0:64], op=ALU.add),
                                reads=[ps_b[obk]], writes=[acc_b])
                            P.op("dve", lambda e, dap=dap, obk=obk: e.tensor_tensor(
                                out=dap, in0=dap, in1=ps[obk][:, 128:192], op=ALU.add),
                                reads=[ps_b[obk]], writes=[acc_b])
                    if g == 2:
                        P.op("dve", lambda e: e.reciprocal(out=acc_d[:], in_=acc_d[:]), reads=[acc_b], writes=[acc_b])
                        P.op("dve", lambda e, j=j: e.tensor_tensor(out=oT[:, j, 0:HALF], in0=acc_o[:], in1=acc_d[:],
                                                                   op=ALU.mult), reads=[acc_b], writes=[oT_b[j]])

                pending_core = None
                for j in range(8):
                    for g in range(3):
                        hh = g * 8 + j
                        Lw, d = GROUPS[g]
                        Lp = 0 if hf == 0 else (Lw if g < 2 else HALF)
                        for which in range(3):
                            wt = next_w("qkv")
                            pbk = PA if which % 2 == 0 else PB
                            gemm(pbk, wt, hacts(T), cts)
                            if which == 0:
                                qknorm(pbk, qgs[:, l:l + 1], qn[g][:, 0:T], qn_b[g], 0, hh)
                            elif which == 1:
                                qknorm(pbk, prm[:, 146 + l:147 + l], kx[g][:, Lp:Lp + T], kx_b[g], 1, hh)
                                src, src_b = nf, nf_b
                            else:
                                for ci, (c0, c1) in enumerate(cts):
                                    P.op("act", lambda e, ci=ci, c0=c0, c1=c1: e.copy(
                                        out=zs[:, c0:c1], in_=ps[pbk[ci]][:, 0:c1 - c0]),
                                        reads=[ps_b[pbk[ci]]], writes=[zs_b])
                                P.op("dve", lambda e: e.tensor_copy(out=vx[g][:, Lp:Lp + T], in_=zs[:, 0:T]),
                                     reads=[zs_b], writes=[vx_b[g]])
                                if hf == 0:
                                    sample_tm(zs, zs_b, 2, hh)
                                src, src_b = zs, zs_b
                            if which >= 1:
                                kvi = which - 1
                                if g == 2:
                                    P.dma("sp", kvo_d[2][l, kvi, j, :, hf * HALF:(hf + 1) * HALF], src[:, 0:HALF],
                                          reads=[src_b])
                                elif hf == 1:
                                    P.dma("sp", kvo_d[g][l, kvi, j, :, :], src[:, HALF - Lw:HALF],
                                          reads=[src_b])
                                xg = kx[g] if which == 1 else vx[g]
                                xg_b = kx_b[g] if which == 1 else vx_b[g]
                                if hf == 0:
                                    P.dma("sp", kvprev_d[hh, kvi], xg[:, 0:HALF], reads=[xg_b, kvprev_b[hh]])
                                else:
                                    P.dma("sp", xg[:, 0:Lp], kvprev_d[hh, kvi][:, HALF - Lp:HALF],
                                          writes=[xg_b, kvprev_b[hh]])
                        if pending_core is not None:
                            attn_core(*pending_core)
                        pending_core = (g, j)
                attn_core(*pending_core)
            scope_fence(fb)
            chk(f"attn_{l}{hf}", [("oT", oT[:], oT_b)])

            if hf == 0:
                with ExitStack() as s:
                    c8 = sb("c8s", [8, 1024], F32, s)
                    c8_b = Buf()
                    P.dma("sp", c8[:], c8_d.ap(), writes=[c8_b])
                    sq_tm = sb("s_q", [8, 3072], F32, s)
                    sk_tm = sb("s_k", [8, 3072], F32, s)
                    sv_tm = sb("s_v", [8, 3072], F32, s)
                    tm_b = Buf()
                    KV = sb("s_KV", [128, 2, 2048], F32, s)
                    KV_b = [Buf(), Buf()]
                    prod = sb("s_prod", [128, 1024], F32, s)
                    prod_b = Buf()
                    Ssc = sb("s_S", [128, 2, 8], F32, s)
                    S_b = [Buf(), Buf()]
                    pself = sb("s_pself", [8, 24], F32, s)
                    sm_b = Buf()
                    osm = sb("s_o", [8, 1024], F32, s)
                    dsm = sb("s_d", [8, 8], F32, s)
                    o16 = sb("s_o16", [128, 8], F32, s)
                    t8 = KV[0:8, :, :].rearrange("p a b -> p (a b)")[:, 0:3072]
                    fb = [tm_b, prod_b, sm_b, c8_b] + KV_b + S_b
                    P.dma("sp", sq_tm[:], snew_d[:, 0].rearrange("t h e -> t (h e)"), writes=[tm_b, snew_b])
                    P.dma("sp", sk_tm[:], snew_d[:, 1].rearrange("t h e -> t (h e)"), writes=[tm_b, snew_b])
                    P.dma("sp", sv_tm[:], snew_d[:, 2].rearrange("t h e -> t (h e)"), writes=[tm_b, snew_b])
                    for g in range(3):
                        L = GROUPS[g][0]
                        P.dma("sp", sk_d[g][l, L - 8:L, 0:1024], sk_tm[:, g * 1024:(g + 1) * 1024],
                              reads=[tm_b])
                        P.dma("sp", sk_d[g][l, L - 8:L, 1024:2048], sv_tm[:, g * 1024:(g + 1) * 1024],
                              reads=[tm_b])
                    P.op("dve", lambda e: e.tensor_tensor(out=t8, in0=sq_tm[:], in1=sk_tm[:], op=ALU.mult),
                         reads=[tm_b], writes=[sm_b] + KV_b)
                    P.op("dve", lambda e: e.tensor_reduce(out=pself[:], in_=t8.rearrange("p (h e) -> p h e", e=128),
                                                          axis=AX.X, op=ALU.add), reads=[sm_b] + KV_b, writes=[sm_b])
                    P.op("dve", lambda e: e.tensor_tensor(out=pself[:], in0=pself[:], in1=b0[:], op=ALU.add),
                         reads=[sm_b, tab_b], writes=[sm_b])
                    P.op("act", lambda e: e.activation(out=pself[:], in_=pself[:], func=AF.Exp), reads=[sm_b], writes=[sm_b])
                    cnt = 0
                    for g in range(3):
                        L, d = GROUPS[g]
                        for i in range(8):
                            k = cnt % 2
                            M0 = sum(1 for m in range(128) if i + m * d < L)
                            src = bass.AP(ck_d[g], (l * L + i) * 2048, [[d * 2048, M0], [1, 2048]])
                            P.dma("sp", KV[0:M0, k, :], src, writes=[KV_b[k]])
                            if M0 < 128:
                                t0 = i + M0 * d - L
                                nn = 128 - M0
                                for kvi in range(2):
                                    srcn = bass.AP(snew_d, (t0 * 3 + 1 + kvi) * 3072 + g * 1024,
                                                   [[d * 9216, nn], [1, 1024]])
                                    P.dma("sp", KV[M0:128, k, kvi * 1024:(kvi + 1) * 1024], srcn,
                                          reads=[snew_b], writes=[KV_b[k]])
                            for n2 in range(2):
                                P.op("pe", lambda e, i=i, g=g, n2=n2: e.matmul(
                                    ps[4 + n2][:, 0:512], lhsT=c8[:, i * 128:(i + 1) * 128],
                                    rhs=sq_tm[:, g * 1024 + n2 * 512:g * 1024 + (n2 + 1) * 512], start=True, stop=True),
                                    reads=[c8_b, tm_b], writes=[ps_b[4 + n2]])
                                P.op("dve", lambda e, k=k, n2=n2: e.tensor_tensor(
                                    out=prod[:, n2 * 512:(n2 + 1) * 512], in0=KV[:, k, n2 * 512:(n2 + 1) * 512],
                                    in1=ps[4 + n2][:, 0:512], op=ALU.mult),
                                    reads=[KV_b[k], ps_b[4 + n2]], writes=[prod_b])
                            P.op("dve", lambda e, k=k: e.tensor_reduce(
                                out=Ssc[:, k, :], in_=prod[:].rearrange("p (h e) -> p h e", e=128), axis=AX.X, op=ALU.add),
                                reads=[prod_b], writes=[S_b[k]])
                            P.op("dve", lambda e, k=k, g=g: e.tensor_tensor(out=Ssc[:, k, :], in0=Ssc[:, k, :],
                                                                           in1=biasT[:, g, :], op=ALU.add),
                                 reads=[S_b[k], tab_b], writes=[S_b[k]])
                            P.op("act", lambda e, k=k: e.activation(out=Ssc[:, k, :], in_=Ssc[:, k, :], func=AF.Exp),
                                 reads=[S_b[k]], writes=[S_b[k]])
                            P.op("dve", lambda e, k=k: e.tensor_tensor(
                                out=prod[:].rearrange("p (h e) -> p h e", e=128),
                                in0=KV[:, k, 1024:2048].rearrange("p (h e) -> p h e", e=128),
                                in1=Ssc[:, k, :].unsqueeze(2).to_broadcast([128, 8, 128]), op=ALU.mult),
                                reads=[KV_b[k], S_b[k]], writes=[prod_b])
                            first_, last_ = (cnt == 0), (cnt == 23)
                            for n2 in range(2):
                                P.op("pe", lambda e, i=i, n2=n2, first_=first_, last_=last_: e.matmul(
                                    ps[n2][0:8, 0:512], lhsT=selT(i), rhs=prod[:, n2 * 512:(n2 + 1) * 512],
                                    start=first_, stop=last_), reads=[cst_b, prod_b], writes=[ps_b[n2]], signal=True)
                            P.op("pe", lambda e, i=i, k=k, first_=first_, last_=last_: e.matmul(
                                ps[2][0:8, 0:8], lhsT=selT(i), rhs=Ssc[:, k, :], start=first_, stop=last_),
                                reads=[cst_b, S_b[k]], writes=[ps_b[2]], signal=True)
                            cnt += 1
                    for n2 in range(2):
                        P.op("act", lambda e, n2=n2: e.copy(out=osm[:, n2 * 512:(n2 + 1) * 512], in_=ps[n2][0:8, 0:512]),
                             reads=[ps_b[n2]], writes=[sm_b])
                    P.op("act", lambda e: e.copy(out=dsm[:], in_=ps[2][0:8, 0:8]), reads=[ps_b[2]], writes=[sm_b])
                    for g in range(3):
                        P.op("dve", lambda e, g=g: e.tensor_tensor(
                            out=t8[:, 0:1024].rearrange("p (h e) -> p h e", e=128),
                            in0=sv_tm[:, g * 1024:(g + 1) * 1024].rearrange("p (h e) -> p h e", e=128),
                            in1=pself[:, g * 8:(g + 1) * 8].unsqueeze(2).to_broadcast([8, 8, 128]), op=ALU.mult),
                            reads=[tm_b, sm_b], writes=[sm_b] + KV_b)
                        P.op("dve", lambda e: e.tensor_tensor(out=osm[:], in0=osm[:], in1=t8[:, 0:1024], op=ALU.add),
                             reads=[sm_b] + KV_b, writes=[sm_b])
                        P.op("dve", lambda e, g=g: e.tensor_tensor(out=dsm[:], in0=dsm[:], in1=pself[:, g * 8:(g + 1) * 8],
                                                                   op=ALU.add), reads=[sm_b], writes=[sm_b])
                    P.op("dve", lambda e: e.reciprocal(out=dsm[:], in_=dsm[:]), reads=[sm_b], writes=[sm_b])
                    P.op("dve", lambda e: e.tensor_tensor(
                        out=osm[:].rearrange("p (h e) -> p h e", e=128), in0=osm[:].rearrange("p (h e) -> p h e", e=128),
                        in1=dsm[:].unsqueeze(2).to_broadcast([8, 8, 128]), op=ALU.mult), reads=[sm_b], writes=[sm_b])
                    for j in range(8):
                        P.op("pe", lambda e, j=j: e.transpose(out=ps[3][:, 0:8], in_=osm[:, j * 128:(j + 1) * 128],
                                                              identity=c128[0:8, 384:392]),
                             reads=[sm_b, cst_b], writes=[ps_b[3]])
                        P.op("act", lambda e, j=j: e.copy(out=oT[:, j, HALF:TA], in_=ps[3][:, 0:8]),
                             reads=[ps_b[3]], writes=[oT_b[j]])
                scope_fence(fb)
            chk(f"samp_{l}{hf}", [("oT", oT[:], oT_b)])

            with ExitStack() as s:
                mg_t = sb("g_m", [128, 1, 16, TA], BF16, s)
                mg_b = [[Buf() for _ in range(16)]]
                sg = sb("g_sg", [128, TA], F32, s)
                m1 = sb("g_m1", [128, TA], F32, s)
                t2 = sb("g_t2", [128, TA], F32, s)
                sg_b, m1_b, t2_b = Buf(), Buf(), Buf()
                fb = [sg_b, m1_b, t2_b] + mg_b[0]
                aacts = [(aT[:, c, :], aT_b[c]) for c in range(8)]
                oacts = [(oT[:, c, :], oT_b[c]) for c in range(8)]
                for mg in range(2):
                    par = 0
                    for c in range(mg * 16, mg * 16 + 16):
                        for half2 in range(2):
                            wt = next_w("ga" if half2 == 0 else "gb")
                            gemm(PA, wt, hacts(T), cts)
                            wt2 = next_w("pa" if half2 == 0 else "pb")
                            gemm(PB, wt2, aacts if half2 == 0 else oacts, cts)
                            for ci, (c0, c1) in enumerate(cts):
                                P.op("act", lambda e, ci=ci, c0=c0, c1=c1: e.activation(
                                    out=sg[:, c0:c1], in_=ps[PA[ci]][:, 0:c1 - c0], func=AF.Sigmoid),
                                    reads=[ps_b[PA[ci]]], writes=[sg_b])
                                if half2 == 0:
                                    P.op("dve", lambda e, ci=ci, c0=c0, c1=c1: e.tensor_tensor(
                                        out=m1[:, c0:c1], in0=sg[:, c0:c1], in1=ps[PB[ci]][:, 0:c1 - c0], op=ALU.mult),
                                        reads=[sg_b, ps_b[PB[ci]]], writes=[m1_b])
                                else:
                                    P.op("dve", lambda e, ci=ci, c0=c0, c1=c1: e.tensor_tensor(
                                        out=t2[:, c0:c1], in0=sg[:, c0:c1], in1=ps[PB[ci]][:, 0:c1 - c0], op=ALU.mult),
                                        reads=[sg_b, ps_b[PB[ci]]], writes=[t2_b])
                                    P.op("dve", lambda e, c0=c0, c1=c1, c=c, par=par: e.tensor_tensor(
                                        out=mg_t[:, par, c % 16, c0:c1], in0=t2[:, c0:c1], in1=m1[:, c0:c1], op=ALU.add),
                                        reads=[t2_b, m1_b], writes=[mg_b[par][c % 16]])
                    macts = [(mg_t[:, par, c, :], mg_b[par][c]) for c in range(16)]

                    def prod_wo(c2, pbk, macts=macts):
                        wt = next_w("wo")
                        gemm(pbk, wt, macts, cts)
                    with ExitStack() as s2:
                        fb2 = residual_loop(NCH, T, off, cts, prod_wo, s2, f"w{mg}")
                    scope_fence(fb2)
            scope_fence(fb)
        scope_fence(mixfence)
        chk(f"mix_{l}{hf}", [])

        with ExitStack() as ffn:
            with ExitStack() as s:
                fb = rmsnorm(l, 1, off, T, cts, s)
            scope_fence(fb)
            hid = sb("f_hid", [128, NCH, TA], BF16, ffn)
            hid_b = [Buf() for _ in range(NCH)]
            rl = sb("f_rl", [128, 2, TA], F32, ffn)
            rl_b = [Buf(), Buf()]
            fb = hid_b + rl_b
            for hg in range(4):
                for jj in range(32):
                    wt = next_w("ff1")
                    pbk = PA if jj % 2 == 0 else PB
                    k = jj % 2
                    gemm(pbk, wt, hacts(T), cts)
                    for ci, (c0, c1) in enumerate(cts):
                        P.op("act", lambda e, ci=ci, c0=c0, c1=c1, k=k, pbk=pbk: e.activation(
                            out=rl[:, k, c0:c1], in_=ps[pbk[ci]][:, 0:c1 - c0], func=AF.Relu),
                            reads=[ps_b[pbk[ci]]], writes=[rl_b[k]])
                    P.op("dve", lambda e, k=k, jj=jj: e.tensor_tensor(out=hid[:, jj, 0:T], in0=rl[:, k, 0:T],
                                                                       in1=rl[:, k, 0:T], op=ALU.mult),
                         reads=[rl_b[k]], writes=[hid_b[jj]])
                hidacts = [(hid[:, c, :], hid_b[c]) for c in range(NCH)]

                def prod_ff2(c2, pbk):
                    wt = next_w("ff2")
                    gemm(pbk, wt, hidacts, cts)
                with ExitStack() as s2:
                    fb2 = residual_loop(NCH, T, off, cts, prod_ff2, s2, f"f{hg}")
                scope_fence(fb2)
        scope_fence(fb)
        pending_barrier.append([])
        chk(f"ffn_{l}{hf}", [])

    try:
        chk("setup", [("tcur", tcur[:], [tab_b]), ("tprev", tprev[:], [tab_b]), ("biasT", biasT[:], [tab_b]),
                      ("b0", b0[:], [tab_b])])
        for l in range(DEPTH):
            for hf in range(2):
                layer_half(l, hf)
    except StopBuild:
        pass
    if not P.dead:
        P.finish()
    st.close()
    nc._dbg_dumps = dumps
    return nc


def make_in_maps(x_prompt, x_sample, state_pool, caches, w_in, w_pool, w_pa, w_pb, w_o, w_ff1, w_ff2,
                 w_norm1, w_norm2, pool_scale, q_norm, k_norm, rel_bias, small_w=False, cores=range(8)):
    ws = [np.zeros(16, np.float32)] * 2 if small_w else [pack_layer(l, w_in, w_pool, w_pa, w_pb, w_o, w_ff1, w_ff2) for l in range(DEPTH)]
    c32, c128, c8 = static_consts()
    prm = np.zeros((128, 148), np.float32)
    for l in range(DEPTH):
        prm[:, l * 64:l * 64 + 32] = w_norm1[l].reshape(32, 128).T
        prm[:, l * 64 + 32:l * 64 + 64] = w_norm2[l].reshape(32, 128).T
        prm[:, 128 + l * 8:128 + l * 8 + 8] = pool_scale[l].reshape(8, 128).T
        prm[:, 144 + l] = q_norm[l]
        prm[:, 146 + l] = k_norm[l]

    in_maps = []
    for c in cores:
        xt = np.zeros((D, XW), np.float32)
        if c in PROMPT_CORES:
            s = PROMPT_CORES.index(c)
            xt[:, 0:HALF] = x_prompt[s, 0:HALF].T
            xt[:, TA:XW] = x_prompt[s, HALF:SEQ].T
        xt[:, HALF:TA] = x_sample[c].T
        m = {"xT": np.ascontiguousarray(xt.reshape(NCH, 128, XW)), "w0": ws[0], "w1": ws[1], "prm": prm,
             "relb": rel_bias, "c32": c32, "c128": c128, "c8": c8,
             "spool": np.ascontiguousarray(state_pool[:, c].reshape(DEPTH, 15, 8, 128).transpose(0, 2, 3, 1))}
        for g in range(3):
            m[f"ck{g}"] = np.ascontiguousarray(caches[g][:, c].reshape(DEPTH, GROUPS[g][0], 2048))
        in_maps.append(m)

    return in_maps


_CACHE = {}


def kernel(x_prompt, x_sample, state_pool, cache_kv_w128, cache_kv_w512, cache_kv_w2048,
           w_norm1, w_in, w_pool, pool_scale, q_norm, k_norm, rel_bias, w_pa, w_pb, w_o,
           w_norm2, w_ff1, w_ff2):
    f = lambda a: np.asarray(a, dtype=np.float32)
    x_prompt, x_sample, state_pool = f(x_prompt), f(x_sample), f(state_pool)
    caches = [f(cache_kv_w128), f(cache_kv_w512), f(cache_kv_w2048)]
    w_in, w_pool, w_pa, w_pb, w_o, w_ff1, w_ff2 = map(f, (w_in, w_pool, w_pa, w_pb, w_o, w_ff1, w_ff2))
    w_norm1, w_norm2, pool_scale, q_norm, k_norm, rel_bias = map(f, (w_norm1, w_norm2, pool_scale, q_norm, k_norm, rel_bias))

    in_maps = make_in_maps(x_prompt, x_sample, state_pool, caches, w_in, w_pool, w_pa, w_pb, w_o, w_ff1, w_ff2,
                           w_norm1, w_norm2, pool_scale, q_norm, k_norm, rel_bias)
    if "nc" not in _CACHE:
        _CACHE["nc"] = build_program()
    nc = _CACHE["nc"]
    res = run_bass_kernel_spmd(nc, in_maps, core_ids=list(range(8)))
    R = res.results

    y_prompt = np.empty((4, SEQ, D), np.float32)
    y_sample = np.empty((8, NS, D), np.float32)
    pool_p = np.empty((DEPTH, 4, 15, 1024), np.float32)
    pool_s = np.empty((DEPTH, 8, 15, 1024), np.float32)
    kv_p = [np.empty((DEPTH, 4, min(GROUPS[g][0], SEQ), 2, 8, 128), np.float32) for g in range(3)]
    kv_s = [np.empty((DEPTH, 8, GROUPS[g][0], 2, 8, 128), np.float32) for g in range(3)]
    for c in range(8):
        r = R[c]
        yt = r["yT"].reshape(D, XW)
        y_sample[c] = yt[:, HALF:TA].T
        pool_s[:, c] = r["pools"].transpose(0, 3, 1, 2).reshape(DEPTH, 15, 1024)
        for g in range(3):
            kv_s[g][:, c] = r[f"sk{g}"].reshape(DEPTH, GROUPS[g][0], 2, 8, 128)
        if c in PROMPT_CORES:
            sq_ = PROMPT_CORES.index(c)
            y_prompt[sq_, 0:HALF] = yt[:, 0:HALF].T
            y_prompt[sq_, HALF:SEQ] = yt[:, TA:XW].T
            pool_p[:, sq_] = r["poolp"].transpose(0, 3, 1, 2).reshape(DEPTH, 15, 1024)
            for g in range(3):
                kv_p[g][:, sq_] = r[f"kvo{g}"].transpose(0, 4, 1, 2, 3)
    return (y_prompt, y_sample, pool_p, kv_p[0], kv_p[1], kv_p[2], pool_s, kv_s[0], kv_s[1], kv_s[2])
```

```python
import math
from contextlib import ExitStack
import numpy as np
import concourse.bass as bass
import concourse.mybir as mybir
from concourse.bass_utils import run_bass_kernel_spmd

F32, BF16 = mybir.dt.float32, mybir.dt.bfloat16
ALU = mybir.AluOpType
AF = mybir.ActivationFunctionType
AX = mybir.AxisListType

D = 4096
NCH = 32
SEQ = 2048
HALF = 1024
NS = 8
TA = HALF + NS
XW = 2 * HALF + NS
DEPTH = 2
GROUPS = ((128, 1), (512, 4), (2048, 16))
EPS = 1e-6
NWSLOT = 3
PROMPT_CORES = [0, 1, 4, 5]
DBG = {}


def t5_bucket_np(dist):
    dist = np.asarray(dist, np.int64)
    nf = np.maximum(dist, 1).astype(np.float32)
    large = 16 + (np.log(nf / np.float32(16)) / np.float32(math.log(2048 / 16)) * np.float32(16)).astype(np.int32)
    large = np.minimum(large, 31)
    return np.where(dist < 16, dist, large).astype(np.int64)


def tile_w(W, k0, KC, n0):
    blk = W[k0:k0 + KC * 128, n0:n0 + 128]
    return np.ascontiguousarray(blk.reshape(KC, 128, 128).transpose(1, 0, 2)).reshape(-1)


def layer_tiles():
    tl = []
    for c in range(8):
        tl.append(("u", c, 32))
    for oc in range(8):
        tl.append(("pool", oc, 2))
    for j in range(8):
        for g in range(3):
            for which in range(3):
                tl.append(("qkv", (g, j, which), 32))
    for mg in range(2):
        for c in range(mg * 16, mg * 16 + 16):
            tl.append(("ga", c, 32))
            tl.append(("pa", c, 8))
            tl.append(("gb", c, 32))
            tl.append(("pb", c, 8))
        for c2 in range(32):
            tl.append(("wo", (mg, c2), 16))
    for hg in range(4):
        for jj in range(32):
            tl.append(("ff1", hg * 32 + jj, 32))
        for c2 in range(32):
            tl.append(("ff2", (hg, c2), 32))
    return tl


TILES = layer_tiles()
TOFF = []
_o = 0
for _k, _a, _kc in TILES:
    TOFF.append(_o)
    _o += 128 * _kc * 128
WLEN = _o


def pack_layer(l, w_in, w_pool, w_pa, w_pb, w_o, w_ff1, w_ff2):
    out = np.empty(WLEN, np.float32)
    Win, Wpa, Wpb, Wo, W1, W2 = w_in[l], w_pa[l], w_pb[l], w_o[l], w_ff1[l], w_ff2[l]
    for (kind, a, kc), off in zip(TILES, TOFF):
        n = 128 * kc * 128
        if kind == "u":
            t = tile_w(Win, 0, 32, a * 128)
        elif kind == "pool":
            g = a // 2
            t = tile_w(w_pool[l, g], 0, 2, (a % 2) * 128)
        elif kind == "qkv":
            g, j, which = a
            t = tile_w(Win, 0, 32, 1024 + which * 3072 + (g * 8 + j) * 128)
        elif kind == "ga":
            t = tile_w(Win, 0, 32, 10240 + a * 128)
        elif kind == "gb":
            t = tile_w(Win, 0, 32, 14336 + a * 128)
        elif kind == "pa":
            t = tile_w(Wpa, 0, 8, a * 128)
        elif kind == "pb":
            t = tile_w(Wpb, 0, 8, a * 128)
        elif kind == "wo":
            t = tile_w(Wo, a[0] * 2048, 16, a[1] * 128)
        elif kind == "ff1":
            t = tile_w(W1, 0, 32, a * 128)
        elif kind == "ff2":
            t = tile_w(W2, a[0] * 4096, 32, a[1] * 128)
        out[off:off + n] = t
    return out


def static_consts():
    c32 = np.zeros((32, 3 * 384 + 3 * 128 + 8), np.float32)
    for g, (w, d) in enumerate(GROUPS):
        for c in range(127, 256):
            c32[t5_bucket_np((c - 127) * d), g * 384 + c] = 1.0
        for m in range(128):
            c32[t5_bucket_np((128 - m) * d), 1152 + g * 128 + m] = 1.0
    c32[0, 1536:1544] = 1.0
    c128 = np.zeros((128, 384 + 128 + 64 + 64), np.float32)
    c128[:, 127:256] = 1.0
    c128[:, 384:512] = np.eye(128, dtype=np.float32)
    for i in range(8):
        c128[:, 512 + i * 8 + i] = 1.0
    for wi, w in enumerate((2, 4, 8, 16)):
        for t in range(16):
            c128[:, 576 + wi * 16 + t] = 1.0 / min(t + 1, w)
    c8 = np.zeros((8, 8, 128), np.float32)
    for i in range(8):
        c8[i, i, :] = 1.0
    return c32, c128, c8.reshape(8, 1024)


class Buf:
    __slots__ = ("lw", "rd")

    def __init__(self):
        self.lw = None
        self.rd = {}


class Prog:
    def __init__(self, nc):
        self.nc = nc
        self.engs = {"pe": nc.tensor, "act": nc.scalar, "dve": nc.vector, "pool": nc.gpsimd, "sp": nc.sync}
        self.esem = {k: nc.alloc_semaphore(name=f"e_{k}") for k in ("pe", "act", "dve")}
        self.ecnt = {k: 0 for k in self.esem}
        self.seen = {k: {} for k in self.engs}
        self.dsems = {"sp": [nc.alloc_semaphore(name=f"dsp{i}") for i in range(20)],
                      "pool": [nc.alloc_semaphore(name=f"dpl{i}") for i in range(6)]}
        self.dval = {}
        self.dnext = {"sp": 0, "pool": 0}
        self.dead = False

    def _deps(self, reads, writes):
        deps = {}

        def add(t):
            if t is not None and deps.get(t[0], 0) < t[1]:
                deps[t[0]] = t[1]
        for b in reads:
            add(b.lw)
        for b in writes:
            add(b.lw)
            for s, v in b.rd.items():
                add((s, v))
        return deps

    def _wait(self, e, deps):
        if self.dead:
            return
        eng = self.engs[e]
        seen = self.seen[e]
        for sem, val in deps.items():
            if e == "pe" and sem is self.esem["pe"]:
                continue
            if seen.get(sem, 0) >= val:
                continue
            eng.wait_ge(sem, val)
            seen[sem] = val

    def _mark(self, t, reads, writes):
        for b in reads:
            if b.rd.get(t[0], 0) < t[1]:
                b.rd[t[0]] = t[1]
        for b in writes:
            b.lw = t
            b.rd = {}

    def op(self, e, fn, reads=(), writes=(), signal=True):
        if self.dead:
            return None
        self._wait(e, self._deps(reads, writes))
        ins = fn(self.engs[e])
        sem = self.esem[e]
        if signal:
            self.ecnt[e] += 1
            ins.then_inc(sem, 1)
            t = (sem, self.ecnt[e])
        else:
            t = (sem, self.ecnt[e] + 1)
        self._mark(t, reads, writes)
        return t

    def dma(self, q, out, in_, reads=(), writes=(), **kw):
        if self.dead:
            return None
        self._wait(q, self._deps(reads, writes))
        sems = self.dsems[q]
        i = self.dnext[q]
        self.dnext[q] = (i + 1) % len(sems)
        sem = sems[i]
        prev = self.dval.get(sem, 0)
        if prev:
            self._wait(q, {sem: prev})
        ins = self.engs[q].dma_start(out=out, in_=in_, **kw)
        ins.then_inc(sem, 16)
        self.dval[sem] = prev + 16
        t = (sem, prev + 16)
        self._mark(t, reads, writes)
        return t

    def finish(self):
        for q in ("sp", "pool"):
            for sem in self.dsems[q]:
                v = self.dval.get(sem, 0)
                if v:
                    self._wait("sp", {sem: v})
        for k, sem in self.esem.items():
            if self.ecnt[k]:
                self._wait("sp", {sem: self.ecnt[k]})


class StopBuild(Exception):
    pass


def build_program(stop=None, small_w=False):
    nc = bass.Bass("TRN2", target_bir_lowering=False)
    P = Prog(nc)
    dumps = []

    def chk(tag, tensors):
        if stop != tag or P.dead:
            return
        for name, ap, bufs in tensors:
            dd = nc.dram_tensor("dbg_" + name, list(ap.shape), ap.dtype, kind="ExternalOutput")
            P.dma("sp", dd.ap(), ap, reads=bufs)
            dumps.append("dbg_" + name)
        P.finish()
        P.dead = True

    def din(name, shape, dt=F32):
        return nc.dram_tensor(name, list(shape), dt, kind="ExternalInput")

    def dout(name, shape, dt=F32):
        return nc.dram_tensor(name, list(shape), dt, kind="ExternalOutput")

    xT = din("xT", [NCH, 128, XW])
    wd = [din(f"w{l}", [16 if small_w else WLEN]) for l in range(DEPTH)]
    prm_d = din("prm", [128, 148])
    relb_d = din("relb", [32, 24])
    c32_d = din("c32", [32, 1544])
    c128_d = din("c128", [128, 640])
    c8_d = din("c8", [8, 1024])
    spool_d = din("spool", [DEPTH, 8, 128, 15])
    ck_d = [din(f"ck{g}", [DEPTH, GROUPS[g][0], 2048]) for g in range(3)]

    yT = dout("yT", [NCH, 128, XW])
    poolp_d = dout("poolp", [DEPTH, 8, 128, 15])
    pools_d = dout("pools", [DEPTH, 8, 128, 15])
    kvo_d = [dout(f"kvo{g}", [DEPTH, 2, 8, 128, GROUPS[g][0]]) for g in range(3)]
    sk_d = [dout(f"sk{g}", [DEPTH, GROUPS[g][0], 2048]) for g in range(3)]

    kvprev_d = nc.dram_tensor("kvprev", [24, 2, 128, HALF], BF16, kind="Internal")
    rtab_d = nc.dram_tensor("rtab", [24, 128, 384], F32, kind="Internal")
    snew_d = nc.dram_tensor("snew", [8, 3, 24, 128], F32, kind="Internal")

    xb = [Buf() for _ in range(NCH)]
    kvprev_b = [Buf() for _ in range(24)]
    snew_b = Buf()
    skout_b = [Buf() for _ in range(3)]
    misc_out = Buf()

    st = ExitStack()

    uniq = [0]

    def sb(name, shape, dt, stack=None):
        uniq[0] += 1
        return (stack or st).enter_context(nc.sbuf_tensor(f"{name}_{uniq[0]}", list(shape), dt))

    def pst_(name, shape, dt, stack=None):
        return (stack or st).enter_context(nc.psum_tensor(name, list(shape), dt))

    hT = sb("hT", [128, NCH, TA], BF16)
    hT_b = [Buf() for _ in range(NCH)]
    wbuf = sb("wbuf", [128, NWSLOT, 32 * 128], BF16)
    wbuf_b = [Buf() for _ in range(NWSLOT)]
    tcur = sb("tcur", [128, 24, 128], BF16)
    tprev = sb("tprev", [128, 24, 128], BF16)
    tab_b = Buf()
    c128 = sb("c128s", [128, 640], F32)
    prm = sb("prms", [128, 148], F32)
    relb = sb("relbs", [32, 24], F32)
    cst_b = Buf()
    identb = sb("identb", [128, 128], BF16)
    onesb = sb("onesb", [128, 128], BF16)
    biasT = sb("biasT", [128, 3, 8], F32)
    b0 = sb("b0", [8, 24], F32)
    qgs = sb("qgs", [128, 2], F32)
    uprev = sb("uprev", [128, 8, 15], F32)
    uprev_b = Buf()

    ps = [pst_(f"ps{i}", [128, 512], F32) for i in range(7)]
    ps_b = [Buf() for _ in range(7)]
    pst = pst_("pst", [128, 1024], BF16)
    pst_b = Buf()
    PA, PB = [0, 1, 2], [3, 4, 5]
    PC = 6

    ident = c128[:, 384:512]
    maskx = c128[:, 0:384]

    def selT(i):
        return c128[:, 512 + i * 8: 512 + i * 8 + 8]

    wseq = [(l, ti) for l in range(DEPTH) for hf in range(2) for ti in range(len(TILES))]
    wstate = {"issued": 0, "next": 0}

    def w_issue_upto(n):
        while wstate["issued"] < min(n, len(wseq)):
            k = wstate["issued"]
            l, ti = wseq[k]
            kc = TILES[ti][2]
            slot = k % NWSLOT
            src = bass.AP(wd[l], TOFF[ti], [[kc * 128, 128], [1, kc * 128]])
            P.dma("pool", wbuf[:, slot, 0:kc * 128], src, writes=[wbuf_b[slot]])
            wstate["issued"] += 1

    class WT:
        pass

    def next_w(expect):
        k = wstate["next"]
        l, ti = wseq[k]
        assert TILES[ti][0] == expect, (TILES[ti], expect)
        w_issue_upto(k + NWSLOT)
        wstate["next"] += 1
        r = WT()
        r.slot = k % NWSLOT
        r.buf = wbuf_b[r.slot]
        r.kc = TILES[ti][2]
        return r

    def gemm(pbanks, wt, acts, cts):
        KC = wt.kc
        for ci, (c0, c1) in enumerate(cts):
            pi = pbanks[ci]
            for kc in range(KC):
                ap, b = acts[kc]
                P.op("pe", lambda e, pi=pi, kc=kc, ap=ap, c0=c0, c1=c1: e.matmul(
                    ps[pi][:, 0:c1 - c0], lhsT=wbuf[:, wt.slot, kc * 128:(kc + 1) * 128], rhs=ap[:, c0:c1],
                    start=(kc == 0), stop=(kc == KC - 1)),
                    reads=[wt.buf, b], writes=[ps_b[pi]], signal=(kc == KC - 1))

    for dst, src in ((c128, c128_d), (prm, prm_d), (relb, relb_d)):
        P.dma("sp", dst[:], src.ap(), writes=[cst_b])
    P.op("dve", lambda e: e.tensor_copy(out=identb[:], in_=ident), reads=[cst_b], writes=[cst_b])
    P.op("dve", lambda e: e.memset(onesb[:], 1.0), writes=[cst_b])
    P.op("dve", lambda e: e.tensor_scalar(out=qgs[:], in0=prm[:, 144:146], scalar1=float(128 ** -0.5), scalar2=None,
                                          op0=ALU.mult), reads=[cst_b], writes=[cst_b])
    for c in range(NCH):
        P.dma("sp", yT[c], xT[c], writes=[xb[c]])
    for l in range(DEPTH):
        for g in range(3):
            L = GROUPS[g][0]
            P.dma("sp", sk_d[g][l, 0:L - 8, :], ck_d[g][l, 8:L, :])

    with ExitStack() as s1:
        c32 = sb("c32s", [32, 1544], F32, s1)
        c32_b = Buf()
        P.dma("sp", c32[:], c32_d.ap(), writes=[c32_b])
        rb_bc = sb("rb_bc", [32, 24, 128], F32, s1)
        ebx = sb("ebx", [128, 2, 384], F32, s1)
        rb_b = Buf()
        ebx_b = [Buf(), Buf()]
        rt_b = [Buf() for _ in range(24)]
        P.op("dve", lambda e: e.tensor_copy(out=rb_bc[:], in_=relb[:].unsqueeze(2).to_broadcast([32, 24, 128])),
             reads=[cst_b], writes=[rb_b])
        for hh in range(24):
            g = hh // 8
            k = hh % 2
            P.op("pe", lambda e, hh=hh, g=g, k=k: e.matmul(ps[k][:, 0:384], lhsT=rb_bc[:, hh, :],
                                                          rhs=c32[:, g * 384:(g + 1) * 384], start=True, stop=True),
                 reads=[rb_b, c32_b], writes=[ps_b[k]])
            P.op("act", lambda e, k=k: e.activation(out=ebx[:, k, :], in_=ps[k][:, 0:384], func=AF.Exp),
                 reads=[ps_b[k]], writes=[ebx_b[k]])
            P.op("dve", lambda e, k=k: e.tensor_tensor(out=ebx[:, k, :], in0=ebx[:, k, :], in1=maskx, op=ALU.mult),
                 reads=[ebx_b[k], cst_b], writes=[ebx_b[k]])
            P.dma("sp", rtab_d[hh], ebx[:, k, :], reads=[ebx_b[k]], writes=[rt_b[hh]])
            base = hh * 128 * 384
            P.dma("pool", tcur[:, hh, :], bass.AP(rtab_d, base + 127, [[383, 128], [1, 128]]),
                  reads=[rt_b[hh]], writes=[tab_b])
            P.dma("pool", tprev[:, hh, :], bass.AP(rtab_d, base + 255, [[383, 128], [1, 128]]),
                  reads=[rt_b[hh]], writes=[tab_b])
        for g in range(3):
            P.op("pe", lambda e, g=g: e.matmul(ps[2][:, 0:8], lhsT=c32[:, 1152 + g * 128:1152 + (g + 1) * 128],
                                               rhs=relb[:, g * 8:(g + 1) * 8], start=True, stop=True),
                 reads=[cst_b, c32_b], writes=[ps_b[2]])
            P.op("act", lambda e, g=g: e.copy(out=biasT[:, g, :], in_=ps[2][:, 0:8]), reads=[ps_b[2]], writes=[tab_b])
        P.op("pe", lambda e: e.matmul(ps[3][0:8, 0:24], lhsT=c32[:, 1536:1544], rhs=relb[:], start=True, stop=True),
             reads=[cst_b, c32_b], writes=[ps_b[3]])
        P.op("act", lambda e: e.copy(out=b0[:], in_=ps[3][0:8, 0:24]), reads=[ps_b[3]], writes=[tab_b])
        scope_bufs = [rb_b, c32_b] + ebx_b
        barrier_bufs = list(scope_bufs)

    pending_barrier = [barrier_bufs]

    def scope_fence(bufs):
        deps = {}
        for b in bufs:
            for t in ([b.lw] if b.lw else []) + list(b.rd.items()):
                if deps.get(t[0], 0) < t[1]:
                    deps[t[0]] = t[1]
        for e in ("pe", "act", "dve", "sp", "pool"):
            P._wait(e, deps)

    def xs(c, off, T):
        return yT[c][:, off:off + T]

    def rmsnorm(l, which, off, T, cts, stack):
        xt = sb("n_xt", [128, 4, TA], F32, stack)
        sq = sb("n_sq", [128, 2, TA], BF16, stack)
        rstd = sb("n_rstd", [128, TA], F32, stack)
        rstd_b = Buf()
        xt_b = [Buf() for _ in range(4)]
        sq_b = [Buf(), Buf()]
        for c in range(NCH):
            k = c % 2
            k4 = c % 4
            P.dma("sp", xt[:, k4, 0:T], xs(c, off, T), reads=[xb[c]], writes=[xt_b[k4]])
            P.op("act", lambda e, k=k, k4=k4: e.activation(out=sq[:, k, 0:T], in_=xt[:, k4, 0:T], func=AF.Square),
                 reads=[xt_b[k4]], writes=[sq_b[k]])
            pass
            for ci, (c0, c1) in enumerate(cts):
                P.op("pe", lambda e, ci=ci, k=k, c0=c0, c1=c1, c=c: e.matmul(
                    ps[PA[ci]][:, 0:c1 - c0], lhsT=onesb[:], rhs=sq[:, k, c0:c1], start=(c == 0), stop=(c == NCH - 1)),
                    reads=[sq_b[k], cst_b], writes=[ps_b[PA[ci]]], signal=(c == NCH - 1 or ci == len(cts) - 1))
            if c == 1:
                pass
        chk("n1b", [("sq", sq[:], sq_b)])
        for ci, (c0, c1) in enumerate(cts):
            P.op("act", lambda e, ci=ci, c0=c0, c1=c1: e.activation(out=rstd[:, c0:c1], in_=ps[PA[ci]][:, 0:c1 - c0],
                                                                   func=AF.Sqrt, bias=epsb[:, 0:1], scale=1.0 / D),
                 reads=[ps_b[PA[ci]], cst_b], writes=[rstd_b])
        chk("n1c", [("rstd", rstd[:], [rstd_b])])
        P.op("dve", lambda e: e.reciprocal(out=rstd[:, 0:T], in_=rstd[:, 0:T]), reads=[rstd_b], writes=[rstd_b])
        chk("n1d", [("rstd", rstd[:], [rstd_b])])
        gcol = l * 64 + which * 32
        for c in range(NCH):
            k = c % 4
            P.dma("sp", xt[:, k, 0:T], xs(c, off, T), reads=[xb[c]], writes=[xt_b[k]])
            P.op("dve", lambda e, k=k, c=c: e.scalar_tensor_tensor(
                out=hT[:, c, 0:T], in0=xt[:, k, 0:T], scalar=prm[:, gcol + c:gcol + c + 1], in1=rstd[:, 0:T],
                op0=ALU.mult, op1=ALU.mult), reads=[xt_b[k], rstd_b, cst_b], writes=[hT_b[c]])
        return xt_b + sq_b + [rstd_b]

    epsb = sb("epsb", [128, 1], F32)
    P.op("dve", lambda e: e.memset(epsb[:], EPS), writes=[cst_b])

    hacts = lambda T: [(hT[:, c, :], hT_b[c]) for c in range(NCH)]

    def residual_loop(n, T, off, cts, produce, stack, tag):
        xt = sb(f"r_xt{tag}", [128, 2, TA], F32, stack)
        xo = sb(f"r_xo{tag}", [128, 2, TA], F32, stack)
        xt_b = [Buf(), Buf()]
        xo_b = [Buf(), Buf()]
        P.dma("sp", xt[:, 0, 0:T], xs(0, off, T), reads=[xb[0]], writes=[xt_b[0]])
        for c2 in range(n):
            k = c2 % 2
            pb = PA if k == 0 else PB
            produce(c2, pb)
            if c2 + 1 < n:
                P.dma("sp", xt[:, 1 - k, 0:T], xs(c2 + 1, off, T), reads=[xb[c2 + 1]], writes=[xt_b[1 - k]])
            for ci, (c0, c1) in enumerate(cts):
                P.op("dve", lambda e, ci=ci, c0=c0, c1=c1, k=k, pb=pb: e.tensor_tensor(
                    out=xo[:, k, c0:c1], in0=xt[:, k, c0:c1], in1=ps[pb[ci]][:, 0:c1 - c0], op=ALU.add),
                    reads=[xt_b[k], ps_b[pb[ci]]], writes=[xo_b[k]])
            P.dma("sp", xs(c2, off, T), xo[:, k, 0:T], reads=[xo_b[k]], writes=[xb[c2]])
        return xt_b + xo_b

    def layer_half(l, hf):
        T = TA if hf == 0 else HALF
        off = 0 if hf == 0 else TA
        cts = [(0, 344), (344, 688), (688, 1032)] if hf == 0 else [(0, 512), (512, 1024)]
        fence = []

        with ExitStack() as mix:
            scope_fence(pending_barrier.pop())
            aT = sb("aT", [128, 8, TA], BF16, mix)
            oT = sb("oT", [128, 8, TA], BF16, mix)
            aT_b = [Buf() for _ in range(8)]
            oT_b = [Buf() for _ in range(8)]
            mixfence = aT_b + oT_b

            with ExitStack() as s:
                fb = rmsnorm(l, 0, off, T, cts, s)
            scope_fence(fb)
            chk(f"norm1_{l}{hf}", [("hT", hT[:], hT_b)])

            with ExitStack() as s:
                E = sb("p_E", [128, 15 + HALF], F32, s)
                Es = sb("p_Es", [128, 32], F32, s)
                Sa = sb("p_Sa", [128, 15 + HALF], F32, s)
                Sb = sb("p_Sb", [128, 15 + HALF], F32, s)
                t16 = sb("p_t16", [128, 16], F32, s)
                pooled = sb("p_pool", [128, 8, TA], BF16, s)
                E_b, Es_b, Sa_b, Sb_b, t16_b = Buf(), Buf(), Buf(), Buf(), Buf()
                pooled_b = [Buf() for _ in range(8)]
                for c in range(8):
                    wt = next_w("u")
                    pbk = PA if c % 2 == 0 else PB
                    gemm(pbk, wt, hacts(T), cts)
                    if hf == 0:
                        P.op("dve", lambda e: e.memset(E[:, 0:15], 0.0), writes=[E_b])
                        P.dma("sp", Es[:, 0:15], spool_d[l, c], writes=[Es_b])
                    else:
                        P.op("dve", lambda e, c=c: e.tensor_copy(out=E[:, 0:15], in_=uprev[:, c, :]),
                             reads=[uprev_b], writes=[E_b])
                    for ci, (c0, c1) in enumerate(cts):
                        cp = min(c1, HALF)
                        P.op("act", lambda e, ci=ci, c0=c0, cp=cp, pbk=pbk: e.copy(
                            out=E[:, 15 + c0:15 + cp], in_=ps[pbk[ci]][:, 0:cp - c0]),
                            reads=[ps_b[pbk[ci]]], writes=[E_b])
                        if c1 > HALF:
                            P.op("act", lambda e, ci=ci, c0=c0, c1=c1, pbk=pbk: e.copy(
                                out=Es[:, 15:15 + NS], in_=ps[pbk[ci]][:, HALF - c0:c1 - c0]),
                                reads=[ps_b[pbk[ci]]], writes=[Es_b])
                    chk("p2a", [("E", E[:], [E_b]), ("Es", Es[:], [Es_b])])
                    gi = c // 2
                    w = (2, 4, 8, 16)[gi]

                    def winsum(src, src_b, N, tagb):
                        cur, cur_b = src, src_b
                        tmps = [(Sa, Sa_b), (Sb, Sb_b)]
                        for s_ in range(gi + 1):
                            sh = 1 << s_
                            lo = (1 << (s_ + 1)) - 1
                            dst, dst_b = tmps[s_ % 2]
                            P.op("dve", lambda e, cur=cur, dst=dst, sh=sh, lo=lo: e.tensor_tensor(
                                out=dst[:, lo:N], in0=cur[:, lo:N], in1=cur[:, lo - sh:N - sh], op=ALU.add),
                                reads=[cur_b], writes=[dst_b])
                            cur, cur_b = dst, dst_b
                        return cur, cur_b

                    S, S_b = winsum(E, E_b, 15 + HALF, "p")
                    P.op("dve", lambda e, S=S, c=c: e.scalar_tensor_tensor(
                        out=pooled[:, c, 0:HALF], in0=S[:, 15:15 + HALF], scalar=1.0 / w, in1=E[:, 15:15 + HALF],
                        op0=ALU.mult, op1=ALU.subtract), reads=[S_b, E_b], writes=[pooled_b[c]])
                    chk("p2b1", [("pooled", pooled[:, 0, :], [pooled_b[0]])])
                    if hf == 0:
                        P.op("dve", lambda e, S=S: e.tensor_tensor(out=t16[:], in0=S[:, 15:31],
                                                                   in1=c128[:, 576 + gi * 16:576 + gi * 16 + 16],
                                                                   op=ALU.mult),
                             reads=[S_b, cst_b], writes=[t16_b])
                        P.op("dve", lambda e, c=c: e.tensor_tensor(out=pooled[:, c, 0:16], in0=t16[:], in1=E[:, 15:31],
                                                                   op=ALU.subtract),
                             reads=[t16_b, E_b], writes=[pooled_b[c]])
                        P.op("dve", lambda e, c=c: e.tensor_copy(out=uprev[:, c, :], in_=E[:, HALF:HALF + 15]),
                             reads=[E_b], writes=[uprev_b])
                        chk("p2b2", [("pooled", pooled[:, 0, :], [pooled_b[0]])])
                        S2, S2_b = winsum(Es, Es_b, 15 + NS, "s")
                        P.op("dve", lambda e, S2=S2, c=c: e.scalar_tensor_tensor(
                            out=pooled[:, c, HALF:TA], in0=S2[:, 15:15 + NS], scalar=1.0 / w, in1=Es[:, 15:15 + NS],
                            op0=ALU.mult, op1=ALU.subtract), reads=[S2_b, Es_b], writes=[pooled_b[c]])
                        chk("p2b3", [("pooled", pooled[:, 0, :], [pooled_b[0]])])
                        P.dma("sp", pools_d[l, c], Es[:, 8:23], reads=[Es_b])
                        chk("p2b4", [("pooled", pooled[:, 0, :], [pooled_b[0]])])
                    else:
                        P.dma("sp", poolp_d[l, c], E[:, HALF:HALF + 15], reads=[E_b])
                    chk("p2b", [("E", E[:], [E_b]), ("pooled", pooled[:, 0, :], [pooled_b[0]])])
                chk("p2c", [("pooled", pooled[:], pooled_b)])
                for oc in range(8):
                    wt = next_w("pool")
                    pbk = PA if oc % 2 == 0 else PB
                    g2 = oc // 2
                    gemm(pbk, wt, [(pooled[:, 2 * g2 + kk, :], pooled_b[2 * g2 + kk]) for kk in range(2)], cts)
                    for ci, (c0, c1) in enumerate(cts):
                        P.op("act", lambda e, ci=ci, c0=c0, c1=c1, pbk=pbk, oc=oc: e.activation(
                            out=aT[:, oc, c0:c1], in_=ps[pbk[ci]][:, 0:c1 - c0], func=AF.Copy,
                            scale=prm[:, 128 + l * 8 + oc:128 + l * 8 + oc + 1]),
                            reads=[ps_b[pbk[ci]], cst_b], writes=[aT_b[oc]])
                fb = [E_b, Es_b, Sa_b, Sb_b, t16_b] + pooled_b
            scope_fence(fb)
            chk(f"pool_{l}{hf}", [("aT", aT[:], aT_b)])

            with ExitStack() as s:
                qn = [sb(f"a_qn{g}", [128, TA], BF16, s) for g in range(3)]
                kx = [sb(f"a_kx{g}", [128, GROUPS[g][0] // (2 if g == 2 else 1) + TA], BF16, s) for g in range(3)]
                vx = [sb(f"a_vx{g}", [128, GROUPS[g][0] // (2 if g == 2 else 1) + TA], BF16, s) for g in range(3)]
                qn_b = [Buf() for _ in range(3)]
                kx_b = [Buf() for _ in range(3)]
                vx_b = [Buf() for _ in range(3)]
                zs = sb("a_zs", [128, TA], F32, s)
                sq = sb("a_sq", [128, TA], BF16, s)
                rs = sb("a_rs", [128, TA], F32, s)
                nf = sb("a_nf", [128, TA], F32, s)
                zs_b, sq_b, rs_b, nf_b = Buf(), Buf(), Buf(), Buf()
                acc_o = sb("a_acco", [128, HALF], F32, s)
                acc_d = sb("a_accd", [128, HALF], F32, s)
                acc_b = Buf()
                vt = sb("a_vt", [128, 17, 128], BF16, s)
                vt_b = [Buf() for _ in range(17)]
                pt = sb("a_pt", [128, 4, 128], BF16, s)
                pt_b = [Buf() for _ in range(4)]
                stm = sb("a_stm", [8, 3, 128], F32, s)
                stm_b = Buf()
                fb = qn_b + kx_b + vx_b + [zs_b, sq_b, rs_b, nf_b, acc_b, stm_b] + vt_b + pt_b

                def qknorm(pbk, gaincol, out16, out16_b, sample_slot, hh):
                    for ci, (c0, c1) in enumerate(cts):
                        P.op("act", lambda e, ci=ci, c0=c0, c1=c1: e.copy(out=zs[:, c0:c1], in_=ps[pbk[ci]][:, 0:c1 - c0]),
                             reads=[ps_b[pbk[ci]]], writes=[zs_b])
                        P.op("act", lambda e, ci=ci, c0=c0, c1=c1: e.activation(out=sq[:, c0:c1],
                                                                               in_=ps[pbk[ci]][:, 0:c1 - c0],
                                                                               func=AF.Square),
                             reads=[ps_b[pbk[ci]]], writes=[sq_b])
                    for ci, (c0, c1) in enumerate(cts):
                        P.op("pe", lambda e, c0=c0, c1=c1: e.matmul(ps[PC][:, 0:c1 - c0], lhsT=onesb[:], rhs=sq[:, c0:c1],
                                                                   start=True, stop=True),
                             reads=[sq_b, cst_b], writes=[ps_b[PC]])
                        P.op("act", lambda e, c0=c0, c1=c1: e.activation(out=rs[:, c0:c1], in_=ps[PC][:, 0:c1 - c0],
                                                                        func=AF.Sqrt, bias=epsb[:, 0:1], scale=1.0 / 128),
                             reads=[ps_b[PC], cst_b], writes=[rs_b])
                    P.op("dve", lambda e: e.reciprocal(out=rs[:, 0:T], in_=rs[:, 0:T]), reads=[rs_b], writes=[rs_b])
                    P.op("dve", lambda e: e.scalar_tensor_tensor(out=nf[:, 0:T], in0=zs[:, 0:T], scalar=gaincol,
                                                                 in1=rs[:, 0:T], op0=ALU.mult, op1=ALU.mult),
                         reads=[zs_b, rs_b, cst_b], writes=[nf_b])
                    P.op("act", lambda e: e.copy(out=out16, in_=nf[:, 0:T]), reads=[nf_b], writes=[out16_b])
                    if hf == 0:
                        sample_tm(nf, nf_b, sample_slot, hh)

                def sample_tm(src, src_b, slot, hh):
                    P.op("pe", lambda e: e.transpose(out=ps[PC][0:8, 0:128], in_=src[:, HALF:TA], identity=ident),
                         reads=[src_b, cst_b], writes=[ps_b[PC]])
                    P.op("act", lambda e: e.copy(out=stm[:, slot, :], in_=ps[PC][0:8, 0:128]),
                         reads=[ps_b[PC]], writes=[stm_b])
                    P.dma("sp", snew_d[:, slot, hh, :], stm[:, slot, :], reads=[stm_b, snew_b])

                def attn_core(g, j):
                    hh = g * 8 + j
                    Lw, d = GROUPS[g]
                    Lp = 0 if hf == 0 else (Lw if g < 2 else HALF)
                    first = (g == 0)
                    sbank = [0, 1]
                    obank = [2, 3]
                    it = 0
                    if g < 2:
                        nblk = HALF // (128 * d)
                        blocks = list(range(-1 if Lp else 0, nblk))
                        vidx = {}
                        for r in range(d):
                            for b in blocks:
                                vi = len(vidx)
                                vidx[(r, b)] = vi
                                st0 = Lp + b * 128 * d + r
                                P.op("pe", lambda e, st0=st0, d=d: e.transpose(
                                    out=pst[:, 0:128], in_=vx[g][:, st0:st0 + 127 * d + 1:d], identity=identb[:]),
                                    reads=[vx_b[g], cst_b], writes=[pst_b])
                                P.op("dve", lambda e, vi=vi: e.tensor_copy(out=vt[:, vi, :], in_=pst[:, 0:128]),
                                     reads=[pst_b], writes=[vt_b[vi]])
                        for r in range(d):
                            for n in range(nblk):
                                sbk = sbank[it % 2]
                                obk = obank[it % 2]
                                it += 1
                                q0 = n * 128 * d + r
                                qap = qn[g][:, q0:q0 + 127 * d + 1:d]
                                kbs = [n] + ([n - 1] if (n - 1) in blocks else [])
                                pts = []
                                for ki, b in enumerate(kbs):
                                    st0 = Lp + b * 128 * d + r
                                    P.op("pe", lambda e, st0=st0, ki=ki, sbk=sbk, qap=qap, d=d: e.matmul(
                                        ps[sbk][:, ki * 128:(ki + 1) * 128],
                                        lhsT=kx[g][:, st0:st0 + 127 * d + 1:d], rhs=qap, start=True, stop=True),
                                        reads=[kx_b[g], qn_b[g]], writes=[ps_b[sbk]])
                                    pi = (it % 2) * 2 + ki
                                    tab = tcur if ki == 0 else tprev
                                    P.op("act", lambda e, pi=pi, ki=ki, sbk=sbk: e.activation(
                                        out=pt[:, pi, :], in_=ps[sbk][:, ki * 128:(ki + 1) * 128], func=AF.Exp),
                                        reads=[ps_b[sbk]], writes=[pt_b[pi]])
                                    P.op("dve", lambda e, pi=pi, tab=tab: e.tensor_tensor(
                                        out=pt[:, pi, :], in0=pt[:, pi, :], in1=tab[:, hh, :], op=ALU.mult),
                                        reads=[pt_b[pi], tab_b], writes=[pt_b[pi]])
                                    pts.append((pi, vidx[(r, b)]))
                                for k2, (pi, vi) in enumerate(pts):
                                    P.op("pe", lambda e, pi=pi, vi=vi, k2=k2, obk=obk, npt=len(pts): e.matmul(
                                        ps[obk][:, 0:128], lhsT=vt[:, vi, :], rhs=pt[:, pi, :],
                                        start=(k2 == 0), stop=(k2 == npt - 1)),
                                        reads=[vt_b[vi], pt_b[pi]], writes=[ps_b[obk]], signal=(k2 == len(pts) - 1))
                                for k2, (pi, vi) in enumerate(pts):
                                    P.op("pe", lambda e, pi=pi, k2=k2, obk=obk, npt=len(pts): e.matmul(
                                        ps[obk][:, 128:256], lhsT=onesb[:], rhs=pt[:, pi, :],
                                        start=(k2 == 0), stop=(k2 == npt - 1)),
                                        reads=[cst_b, pt_b[pi]], writes=[ps_b[obk]], signal=(k2 == len(pts) - 1))
                                oap = acc_o[:, q0:q0 + 127 * d + 1:d]
                                dap = acc_d[:, q0:q0 + 127 * d + 1:d]
                                if first:
                                    P.op("act", lambda e, oap=oap, obk=obk: e.copy(out=oap, in_=ps[obk][:, 0:128]),
                                         reads=[ps_b[obk]], writes=[acc_b])
                                    P.op("act", lambda e, dap=dap, obk=obk: e.copy(out=dap, in_=ps[obk][:, 128:256]),
                                         reads=[ps_b[obk]], writes=[acc_b])
                                else:
                                    P.op("dve", lambda e, oap=oap, obk=obk: e.tensor_tensor(
                                        out=oap, in0=oap, in1=ps[obk][:, 0:128], op=ALU.add),
                                        reads=[ps_b[obk]], writes=[acc_b])
                                    P.op("dve", lambda e, dap=dap, obk=obk: e.tensor_tensor(
                                        out=dap, in0=dap, in1=ps[obk][:, 128:256], op=ALU.add),
                                        reads=[ps_b[obk]], writes=[acc_b])
                    else:
                        nk = 64 + Lp // 16
                        toff = 0 if hf == 0 else 64
                        for r in range(16):
                            sbk = sbank[it % 2]
                            obk = obank[it % 2]
                            pi = it % 4
                            vi = it % 16
                            it += 1
                            kap = kx[g][:, r:r + (nk - 1) * 16 + 1:16]
                            vap = vx[g][:, r:r + (nk - 1) * 16 + 1:16]
                            qap = qn[g][:, r:r + 63 * 16 + 1:16]
                            P.op("pe", lambda e, vap=vap: e.transpose(out=pst[0:nk, 0:128], in_=vap, identity=identb[:]),
                                 reads=[vx_b[g], cst_b], writes=[pst_b])
                            P.op("dve", lambda e, vi=vi: e.tensor_copy(out=vt[0:nk, vi, :], in_=pst[0:nk, 0:128]),
                                 reads=[pst_b], writes=[vt_b[vi]])
                            P.op("pe", lambda e, kap=kap, qap=qap, sbk=sbk: e.matmul(
                                ps[sbk][0:nk, 0:64], lhsT=kap, rhs=qap, start=True, stop=True),
                                reads=[kx_b[g], qn_b[g]], writes=[ps_b[sbk]])
                            P.op("act", lambda e, pi=pi, sbk=sbk: e.activation(
                                out=pt[0:nk, pi, 0:64], in_=ps[sbk][0:nk, 0:64], func=AF.Exp),
                                reads=[ps_b[sbk]], writes=[pt_b[pi]])
                            P.op("dve", lambda e, pi=pi: e.tensor_tensor(
                                out=pt[0:nk, pi, 0:64], in0=pt[0:nk, pi, 0:64], in1=tcur[0:nk, hh, toff:toff + 64],
                                op=ALU.mult), reads=[pt_b[pi], tab_b], writes=[pt_b[pi]])
                            P.op("pe", lambda e, pi=pi, vi=vi, obk=obk: e.matmul(
                                ps[obk][:, 0:64], lhsT=vt[0:nk, vi, :], rhs=pt[0:nk, pi, 0:64], start=True, stop=True),
                                reads=[vt_b[vi], pt_b[pi]], writes=[ps_b[obk]])
                            P.op("pe", lambda e, pi=pi, obk=obk: e.matmul(
                                ps[obk][:, 128:192], lhsT=onesb[0:nk, :], rhs=pt[0:nk, pi, 0:64], start=True, stop=True),
                                reads=[cst_b, pt_b[pi]], writes=[ps_b[obk]])
                            oap = acc_o[:, r:r + 63 * 16 + 1:16]
                            dap = acc_d[:, r:r + 63 * 16 + 1:16]
                            P.op("dve", lambda e, oap=oap, obk=obk: e.tensor_tensor(
                                out=oap, in0=oap, in1=ps[obk][:, 0:64], op=ALU.add),
                                reads=[ps_b[obk]], writes=[acc_b])
                            P.op("dve", lambda e, dap=dap, obk=obk: e.tensor_tensor(
                                out=dap, in0=dap, in1=ps[obk][:, 128:192], op=ALU.add),
                                reads=[ps_b[obk]], writes=[acc_b])
                    if g == 2:
                        P.op("dve", lambda e: e.reciprocal(out=acc_d[:], in_=acc_d[:]), reads=[acc_b], writes=[acc_b])
                        P.op("dve", lambda e, j=j: e.tensor_tensor(out=oT[:, j, 0:HALF], in0=acc_o[:], in1=acc_d[:],
                                                                   op=ALU.mult), reads=[acc_b], writes=[oT_b[j]])

                pending_core = None
                for j in range(8):
                    for g in range(3):
                        hh = g * 8 + j
                        Lw, d = GROUPS[g]
                        Lp = 0 if hf == 0 else (Lw if g < 2 else HALF)
                        for which in range(3):
                            wt = next_w("qkv")
                            pbk = PA if which % 2 == 0 else PB
                            gemm(pbk, wt, hacts(T), cts)
                            if which == 0:
                                qknorm(pbk, qgs[:, l:l + 1], qn[g][:, 0:T], qn_b[g], 0, hh)
                            elif which == 1:
                                qknorm(pbk, prm[:, 146 + l:147 + l], kx[g][:, Lp:Lp + T], kx_b[g], 1, hh)
                                src, src_b = nf, nf_b
                            else:
                                for ci, (c0, c1) in enumerate(cts):
                                    P.op("act", lambda e, ci=ci, c0=c0, c1=c1: e.copy(
                                        out=zs[:, c0:c1], in_=ps[pbk[ci]][:, 0:c1 - c0]),
                                        reads=[ps_b[pbk[ci]]], writes=[zs_b])
                                P.op("dve", lambda e: e.tensor_copy(out=vx[g][:, Lp:Lp + T], in_=zs[:, 0:T]),
                                     reads=[zs_b], writes=[vx_b[g]])
                                if hf == 0:
                                    sample_tm(zs, zs_b, 2, hh)
                                src, src_b = zs, zs_b
                            if which >= 1:
                                kvi = which - 1
                                if g == 2:
                                    P.dma("sp", kvo_d[2][l, kvi, j, :, hf * HALF:(hf + 1) * HALF], src[:, 0:HALF],
                                          reads=[src_b])
                                elif hf == 1:
                                    P.dma("sp", kvo_d[g][l, kvi, j, :, :], src[:, HALF - Lw:HALF],
                                          reads=[src_b])
                                xg = kx[g] if which == 1 else vx[g]
                                xg_b = kx_b[g] if which == 1 else vx_b[g]
                                if hf == 0:
                                    P.dma("sp", kvprev_d[hh, kvi], xg[:, 0:HALF], reads=[xg_b, kvprev_b[hh]])
                                else:
                                    P.dma("sp", xg[:, 0:Lp], kvprev_d[hh, kvi][:, HALF - Lp:HALF],
                                          writes=[xg_b, kvprev_b[hh]])
                        if pending_core is not None:
                            attn_core(*pending_core)
                        pending_core = (g, j)
                attn_core(*pending_core)
            scope_fence(fb)
            chk(f"attn_{l}{hf}", [("oT", oT[:], oT_b)])

            if hf == 0:
                with ExitStack() as s:
                    c8 = sb("c8s", [8, 1024], F32, s)
                    c8_b = Buf()
                    P.dma("sp", c8[:], c8_d.ap(), writes=[c8_b])
                    sq_tm = sb("s_q", [8, 3072], F32, s)
                    sk_tm = sb("s_k", [8, 3072], F32, s)
                    sv_tm = sb("s_v", [8, 3072], F32, s)
                    tm_b = Buf()
                    KV = sb("s_KV", [128, 2, 2048], F32, s)
                    KV_b = [Buf(), Buf()]
                    prod = sb("s_prod", [128, 1024], F32, s)
                    prod_b = Buf()
                    Ssc = sb("s_S", [128, 2, 8], F32, s)
                    S_b = [Buf(), Buf()]
                    pself = sb("s_pself", [8, 24], F32, s)
                    sm_b = Buf()
                    osm = sb("s_o", [8, 1024], F32, s)
                    dsm = sb("s_d", [8, 8], F32, s)
                    o16 = sb("s_o16", [128, 8], F32, s)
                    t8 = KV[0:8, :, :].rearrange("p a b -> p (a b)")[:, 0:3072]
                    fb = [tm_b, prod_b, sm_b, c8_b] + KV_b + S_b
                    P.dma("sp", sq_tm[:], snew_d[:, 0].rearrange("t h e -> t (h e)"), writes=[tm_b, snew_b])
                    P.dma("sp", sk_tm[:], snew_d[:, 1].rearrange("t h e -> t (h e)"), writes=[tm_b, snew_b])
                    P.dma("sp", sv_tm[:], snew_d[:, 2].rearrange("t h e -> t (h e)"), writes=[tm_b, snew_b])
                    for g in range(3):
                        L = GROUPS[g][0]
                        P.dma("sp", sk_d[g][l, L - 8:L, 0:1024], sk_tm[:, g * 1024:(g + 1) * 1024],
                              reads=[tm_b])
                        P.dma("sp", sk_d[g][l, L - 8:L, 1024:2048], sv_tm[:, g * 1024:(g + 1) * 1024],
                              reads=[tm_b])
                    P.op("dve", lambda e: e.tensor_tensor(out=t8, in0=sq_tm[:], in1=sk_tm[:], op=ALU.mult),
                         reads=[tm_b], writes=[sm_b] + KV_b)
                    P.op("dve", lambda e: e.tensor_reduce(out=pself[:], in_=t8.rearrange("p (h e) -> p h e", e=128),
                                                          axis=AX.X, op=ALU.add), reads=[sm_b] + KV_b, writes=[sm_b])
                    P.op("dve", lambda e: e.tensor_tensor(out=pself[:], in0=pself[:], in1=b0[:], op=ALU.add),
                         reads=[sm_b, tab_b], writes=[sm_b])
                    P.op("act", lambda e: e.activation(out=pself[:], in_=pself[:], func=AF.Exp), reads=[sm_b], writes=[sm_b])
                    cnt = 0
                    for g in range(3):
                        L, d = GROUPS[g]
                        for i in range(8):
                            k = cnt % 2
                            M0 = sum(1 for m in range(128) if i + m * d < L)
                            src = bass.AP(ck_d[g], (l * L + i) * 2048, [[d * 2048, M0], [1, 2048]])
                            P.dma("sp", KV[0:M0, k, :], src, writes=[KV_b[k]])
                            if M0 < 128:
                                t0 = i + M0 * d - L
                                nn = 128 - M0
                                for kvi in range(2):
                                    srcn = bass.AP(snew_d, (t0 * 3 + 1 + kvi) * 3072 + g * 1024,
                                                   [[d * 9216, nn], [1, 1024]])
                                    P.dma("sp", KV[M0:128, k, kvi * 1024:(kvi + 1) * 1024], srcn,
                                          reads=[snew_b], writes=[KV_b[k]])
                            for n2 in range(2):
                                P.op("pe", lambda e, i=i, g=g, n2=n2: e.matmul(
                                    ps[4 + n2][:, 0:512], lhsT=c8[:, i * 128:(i + 1) * 128],
                                    rhs=sq_tm[:, g * 1024 + n2 * 512:g * 1024 + (n2 + 1) * 512], start=True, stop=True),
                                    reads=[c8_b, tm_b], writes=[ps_b[4 + n2]])
                                P.op("dve", lambda e, k=k, n2=n2: e.tensor_tensor(
                                    out=prod[:, n2 * 512:(n2 + 1) * 512], in0=KV[:, k, n2 * 512:(n2 + 1) * 512],
                                    in1=ps[4 + n2][:, 0:512], op=ALU.mult),
                                    reads=[KV_b[k], ps_b[4 + n2]], writes=[prod_b])
                            P.op("dve", lambda e, k=k: e.tensor_reduce(
                                out=Ssc[:, k, :], in_=prod[:].rearrange("p (h e) -> p h e", e=128), axis=AX.X, op=ALU.add),
                                reads=[prod_b], writes=[S_b[k]])
                            P.op("dve", lambda e, k=k, g=g: e.tensor_tensor(out=Ssc[:, k, :], in0=Ssc[:, k, :],
                                                                           in1=biasT[:, g, :], op=ALU.add),
                                 reads=[S_b[k], tab_b], writes=[S_b[k]])
                            P.op("act", lambda e, k=k: e.activation(out=Ssc[:, k, :], in_=Ssc[:, k, :], func=AF.Exp),
                                 reads=[S_b[k]], writes=[S_b[k]])
                            P.op("dve", lambda e, k=k: e.tensor_tensor(
                                out=prod[:].rearrange("p (h e) -> p h e", e=128),
                                in0=KV[:, k, 1024:2048].rearrange("p (h e) -> p h e", e=128),
                                in1=Ssc[:, k, :].unsqueeze(2).to_broadcast([128, 8, 128]), op=ALU.mult),
                                reads=[KV_b[k], S_b[k]], writes=[prod_b])
                            first_, last_ = (cnt == 0), (cnt == 23)
                            for n2 in range(2):
                                P.op("pe", lambda e, i=i, n2=n2, first_=first_, last_=last_: e.matmul(
                                    ps[n2][0:8, 0:512], lhsT=selT(i), rhs=prod[:, n2 * 512:(n2 + 1) * 512],
                                    start=first_, stop=last_), reads=[cst_b, prod_b], writes=[ps_b[n2]], signal=True)
                            P.op("pe", lambda e, i=i, k=k, first_=first_, last_=last_: e.matmul(
                                ps[2][0:8, 0:8], lhsT=selT(i), rhs=Ssc[:, k, :], start=first_, stop=last_),
                                reads=[cst_b, S_b[k]], writes=[ps_b[2]], signal=True)
                            cnt += 1
                    for n2 in range(2):
                        P.op("act", lambda e, n2=n2: e.copy(out=osm[:, n2 * 512:(n2 + 1) * 512], in_=ps[n2][0:8, 0:512]),
                             reads=[ps_b[n2]], writes=[sm_b])
                    P.op("act", lambda e: e.copy(out=dsm[:], in_=ps[2][0:8, 0:8]), reads=[ps_b[2]], writes=[sm_b])
                    for g in range(3):
                        P.op("dve", lambda e, g=g: e.tensor_tensor(
                            out=t8[:, 0:1024].rearrange("p (h e) -> p h e", e=128),
                            in0=sv_tm[:, g * 1024:(g + 1) * 1024].rearrange("p (h e) -> p h e", e=128),
                            in1=pself[:, g * 8:(g + 1) * 8].unsqueeze(2).to_broadcast([8, 8, 128]), op=ALU.mult),
                            reads=[tm_b, sm_b], writes=[sm_b] + KV_b)
                        P.op("dve", lambda e: e.tensor_tensor(out=osm[:], in0=osm[:], in1=t8[:, 0:1024], op=ALU.add),
                             reads=[sm_b] + KV_b, writes=[sm_b])
                        P.op("dve", lambda e, g=g: e.tensor_tensor(out=dsm[:], in0=dsm[:], in1=pself[:, g * 8:(g + 1) * 8],
                                                                   op=ALU.add), reads=[sm_b], writes=[sm_b])
                    P.op("dve", lambda e: e.reciprocal(out=dsm[:], in_=dsm[:]), reads=[sm_b], writes=[sm_b])
                    P.op("dve", lambda e: e.tensor_tensor(
                        out=osm[:].rearrange("p (h e) -> p h e", e=128), in0=osm[:].rearrange("p (h e) -> p h e", e=128),
                        in1=dsm[:].unsqueeze(2).to_broadcast([8, 8, 128]), op=ALU.mult), reads=[sm_b], writes=[sm_b])
                    for j in range(8):
                        P.op("pe", lambda e, j=j: e.transpose(out=ps[3][:, 0:8], in_=osm[:, j * 128:(j + 1) * 128],
                                                              identity=c128[0:8, 384:392]),
                             reads=[sm_b, cst_b], writes=[ps_b[3]])
                        P.op("act", lambda e, j=j: e.copy(out=oT[:, j, HALF:TA], in_=ps[3][:, 0:8]),
                             reads=[ps_b[3]], writes=[oT_b[j]])
                scope_fence(fb)
            chk(f"samp_{l}{hf}", [("oT", oT[:], oT_b)])

            with ExitStack() as s:
                mg_t = sb("g_m", [128, 1, 16, TA], BF16, s)
                mg_b = [[Buf() for _ in range(16)]]
                sg = sb("g_sg", [128, TA], F32, s)
                m1 = sb("g_m1", [128, TA], F32, s)
                t2 = sb("g_t2", [128, TA], F32, s)
                sg_b, m1_b, t2_b = Buf(), Buf(), Buf()
                fb = [sg_b, m1_b, t2_b] + mg_b[0]
                aacts = [(aT[:, c, :], aT_b[c]) for c in range(8)]
                oacts = [(oT[:, c, :], oT_b[c]) for c in range(8)]
                for mg in range(2):
                    par = 0
                    for c in range(mg * 16, mg * 16 + 16):
                        for half2 in range(2):
                            wt = next_w("ga" if half2 == 0 else "gb")
                            gemm(PA, wt, hacts(T), cts)
                            wt2 = next_w("pa" if half2 == 0 else "pb")
                            gemm(PB, wt2, aacts if half2 == 0 else oacts, cts)
                            for ci, (c0, c1) in enumerate(cts):
                                P.op("act", lambda e, ci=ci, c0=c0, c1=c1: e.activation(
                                    out=sg[:, c0:c1], in_=ps[PA[ci]][:, 0:c1 - c0], func=AF.Sigmoid),
                                    reads=[ps_b[PA[ci]]], writes=[sg_b])
                                if half2 == 0:
                                    P.op("dve", lambda e, ci=ci, c0=c0, c1=c1: e.tensor_tensor(
                                        out=m1[:, c0:c1], in0=sg[:, c0:c1], in1=ps[PB[ci]][:, 0:c1 - c0], op=ALU.mult),
                                        reads=[sg_b, ps_b[PB[ci]]], writes=[m1_b])
                                else:
                                    P.op("dve", lambda e, ci=ci, c0=c0, c1=c1: e.tensor_tensor(
                                        out=t2[:, c0:c1], in0=sg[:, c0:c1], in1=ps[PB[ci]][:, 0:c1 - c0], op=ALU.mult),
                                        reads=[sg_b, ps_b[PB[ci]]], writes=[t2_b])
                                    P.op("dve", lambda e, c0=c0, c1=c1, c=c, par=par: e.tensor_tensor(
                                        out=mg_t[:, par, c % 16, c0:c1], in0=t2[:, c0:c1], in1=m1[:, c0:c1], op=ALU.add),
                                        reads=[t2_b, m1_b], writes=[mg_b[par][c % 16]])
                    macts = [(mg_t[:, par, c, :], mg_b[par][c]) for c in range(16)]

                    def prod_wo(c2, pbk, macts=macts):
                        wt = next_w("wo")
                        gemm(pbk, wt, macts, cts)
                    with ExitStack() as s2:
                        fb2 = residual_loop(NCH, T, off, cts, prod_wo, s2, f"w{mg}")
                    scope_fence(fb2)
            scope_fence(fb)
        scope_fence(mixfence)
        chk(f"mix_{l}{hf}", [])

        with ExitStack() as ffn:
            with ExitStack() as s:
                fb = rmsnorm(l, 1, off, T, cts, s)
            scope_fence(fb)
            hid = sb("f_hid", [128, NCH, TA], BF16, ffn)
            hid_b = [Buf() for _ in range(NCH)]
            rl = sb("f_rl", [128, 2, TA], F32, ffn)
            rl_b = [Buf(), Buf()]
            fb = hid_b + rl_b
            for hg in range(4):
                for jj in range(32):
                    wt = next_w("ff1")
                    pbk = PA if jj % 2 == 0 else PB
                    k = jj % 2
                    gemm(pbk, wt, hacts(T), cts)
                    for ci, (c0, c1) in enumerate(cts):
                        P.op("act", lambda e, ci=ci, c0=c0, c1=c1, k=k, pbk=pbk: e.activation(
                            out=rl[:, k, c0:c1], in_=ps[pbk[ci]][:, 0:c1 - c0], func=AF.Relu),
                            reads=[ps_b[pbk[ci]]], writes=[rl_b[k]])
                    P.op("dve", lambda e, k=k, jj=jj: e.tensor_tensor(out=hid[:, jj, 0:T], in0=rl[:, k, 0:T],
                                                                       in1=rl[:, k, 0:T], op=ALU.mult),
                         reads=[rl_b[k]], writes=[hid_b[jj]])
                hidacts = [(hid[:, c, :], hid_b[c]) for c in range(NCH)]

                def prod_ff2(c2, pbk):
                    wt = next_w("ff2")
                    gemm(pbk, wt, hidacts, cts)
                with ExitStack() as s2:
                    fb2 = residual_loop(NCH, T, off, cts, prod_ff2, s2, f"f{hg}")
                scope_fence(fb2)
        scope_fence(fb)
        pending_barrier.append([])
        chk(f"ffn_{l}{hf}", [])

    try:
        chk("setup", [("tcur", tcur[:], [tab_b]), ("tprev", tprev[:], [tab_b]), ("biasT", biasT[:], [tab_b]),
                      ("b0", b0[:], [tab_b])])
        for l in range(DEPTH):
            for hf in range(2):
                layer_half(l, hf)
    except StopBuild:
        pass
    if not P.dead:
        P.finish()
    st.close()
    nc._dbg_dumps = dumps
    return nc


def make_in_maps(x_prompt, x_sample, state_pool, caches, w_in, w_pool, w_pa, w_pb, w_o, w_ff1, w_ff2,
                 w_norm1, w_norm2, pool_scale, q_norm, k_norm, rel_bias, small_w=False, cores=range(8)):
    ws = [np.zeros(16, np.float32)] * 2 if small_w else [pack_layer(l, w_in, w_pool, w_pa, w_pb, w_o, w_ff1, w_ff2) for l in range(DEPTH)]
    c32, c128, c8 = static_consts()
    prm = np.zeros((128, 148), np.float32)
    for l in range(DEPTH):
        prm[:, l * 64:l * 64 + 32] = w_norm1[l].reshape(32, 128).T
        prm[:, l * 64 + 32:l * 64 + 64] = w_norm2[l].reshape(32, 128).T
        prm[:, 128 + l * 8:128 + l * 8 + 8] = pool_scale[l].reshape(8, 128).T
        prm[:, 144 + l] = q_norm[l]
        prm[:, 146 + l] = k_norm[l]

    in_maps = []
    for c in cores:
        xt = np.zeros((D, XW), np.float32)
        if c in PROMPT_CORES:
            s = PROMPT_CORES.index(c)
            xt[:, 0:HALF] = x_prompt[s, 0:HALF].T
            xt[:, TA:XW] = x_prompt[s, HALF:SEQ].T
        xt[:, HALF:TA] = x_sample[c].T
        m = {"xT": np.ascontiguousarray(xt.reshape(NCH, 128, XW)), "w0": ws[0], "w1": ws[1], "prm": prm,
             "relb": rel_bias, "c32": c32, "c128": c128, "c8": c8,
             "spool": np.ascontiguousarray(state_pool[:, c].reshape(DEPTH, 15, 8, 128).transpose(0, 2, 3, 1))}
        for g in range(3):
            m[f"ck{g}"] = np.ascontiguousarray(caches[g][:, c].reshape(DEPTH, GROUPS[g][0], 2048))
        in_maps.append(m)

    return in_maps


_CACHE = {}


def kernel(x_prompt, x_sample, state_pool, cache_kv_w128, cache_kv_w512, cache_kv_w2048,
           w_norm1, w_in, w_pool, pool_scale, q_norm, k_norm, rel_bias, w_pa, w_pb, w_o,
           w_norm2, w_ff1, w_ff2):
    f = lambda a: np.asarray(a, dtype=np.float32)
    x_prompt, x_sample, state_pool = f(x_prompt), f(x_sample), f(state_pool)
    caches = [f(cache_kv_w128), f(cache_kv_w512), f(cache_kv_w2048)]
    w_in, w_pool, w_pa, w_pb, w_o, w_ff1, w_ff2 = map(f, (w_in, w_pool, w_pa, w_pb, w_o, w_ff1, w_ff2))
    w_norm1, w_norm2, pool_scale, q_norm, k_norm, rel_bias = map(f, (w_norm1, w_norm2, pool_scale, q_norm, k_norm, rel_bias))

    in_maps = make_in_maps(x_prompt, x_sample, state_pool, caches, w_in, w_pool, w_pa, w_pb, w_o, w_ff1, w_ff2,
                           w_norm1, w_norm2, pool_scale, q_norm, k_norm, rel_bias)
    if "nc" not in _CACHE:
        _CACHE["nc"] = build_program()
    nc = _CACHE["nc"]
    res = run_bass_kernel_spmd(nc, in_maps, core_ids=list(range(8)))
    R = res.results

    y_prompt = np.empty((4, SEQ, D), np.float32)
    y_sample = np.empty((8, NS, D), np.float32)
    pool_p = np.empty((DEPTH, 4, 15, 1024), np.float32)
    pool_s = np.empty((DEPTH, 8, 15, 1024), np.float32)
    kv_p = [np.empty((DEPTH, 4, min(GROUPS[g][0], SEQ), 2, 8, 128), np.float32) for g in range(3)]
    kv_s = [np.empty((DEPTH, 8, GROUPS[g][0], 2, 8, 128), np.float32) for g in range(3)]
    for c in range(8):
        r = R[c]
        yt = r["yT"].reshape(D, XW)
        y_sample[c] = yt[:, HALF:TA].T
        pool_s[:, c] = r["pools"].transpose(0, 3, 1, 2).reshape(DEPTH, 15, 1024)
        for g in range(3):
            kv_s[g][:, c] = r[f"sk{g}"].reshape(DEPTH, GROUPS[g][0], 2, 8, 128)
        if c in PROMPT_CORES:
            sq_ = PROMPT_CORES.index(c)
            y_prompt[sq_, 0:HALF] = yt[:, 0:HALF].T
            y_prompt[sq_, HALF:SEQ] = yt[:, TA:XW].T
            pool_p[:, sq_] = r["poolp"].transpose(0, 3, 1, 2).reshape(DEPTH, 15, 1024)
            for g in range(3):
                kv_p[g][:, sq_] = r[f"kvo{g}"].transpose(0, 4, 1, 2, 3)
    return (y_prompt, y_sample, pool_p, kv_p[0], kv_p[1], kv_p[2], pool_s, kv_s[0], kv_s[1], kv_s[2])
```
